# Optimizing a Trainium2 kernel written in Bass

```python
import jax, jax.numpy as jnp
from jax import lax
import numpy as np

D_MODEL = 2048
BATCH = 16
SEQ = 2048
DEPTH = 2

CHUNK = 64
LEFT_CHUNKS = 8
BAND_CHUNKS = LEFT_CHUNKS + 1
D_MIX = D_MODEL
D_LRU = D_MIX // 2
LRU_HEADS = 16
LRU_BLOCK = D_LRU // LRU_HEADS
CONV_WIDTH = 4
LRU_C = 8.0
D_ATT = D_MIX - D_LRU
ATT_HEADS = 8
HEAD_DIM = D_ATT // ATT_HEADS
MAX_REL = 128
N_REL = 2 * MAX_REL + 1
D_FF = 5632
D_IN = 2 * D_LRU + 3 * D_ATT
EPS = 1e-6
NEG_INF = -1e30

kernel_name = "macaron_hybrid_rglru_chunkattn"


def rmsnorm(x, g):
    xf = x.astype(jnp.float32)
    y = xf * lax.rsqrt(jnp.mean(xf * xf, axis=-1, keepdims=True) + EPS)
    return (y * g.astype(jnp.float32)).astype(x.dtype)


def swiglu(h, w_gate, w_up, w_down):
    return (jax.nn.silu(h @ w_gate) * (h @ w_up)) @ w_down


def causal_dwconv(x, w, b):
    s = x.shape[1]
    xp = jnp.pad(x, ((0, 0), (CONV_WIDTH - 1, 0), (0, 0)))
    y = b
    for k in range(CONV_WIDTH):
        y = y + xp[:, k:k + s] * w[k]
    return y


def block_diag_linear(x, w, b):
    bsz, s, _ = x.shape
    xh = x.reshape(bsz, s, LRU_HEADS, LRU_BLOCK)
    return jnp.einsum('bshi,hij->bshj', xh, w).reshape(bsz, s, D_LRU) + b


def _lin_rec_combine(e1, e2):
    a1, b1 = e1
    a2, b2 = e2
    return a1 * a2, a2 * b1 + b2


def rg_lru(x, w_a, b_a, w_x, b_x, lam):
    r = jax.nn.sigmoid(block_diag_linear(x, w_a, b_a)).astype(jnp.float32)
    i = jax.nn.sigmoid(block_diag_linear(x, w_x, b_x)).astype(jnp.float32)
    log_a = -LRU_C * r * jax.nn.softplus(-lam.astype(jnp.float32))
    a = jnp.exp(log_a)
    mult = jnp.sqrt(-jnp.expm1(2.0 * log_a))
    u = mult * i * x.astype(jnp.float32)
    _, h = lax.associative_scan(_lin_rec_combine, (a, u), axis=1)
    return h.astype(x.dtype)


def chunk_attention(q, k, v, rel_table):
    bsz, s, _ = q.shape
    nc = s // CHUNK
    q = q.reshape(bsz, nc, CHUNK, ATT_HEADS, HEAD_DIM)
    k = k.reshape(bsz, nc, CHUNK, ATT_HEADS, HEAD_DIM)
    v = v.reshape(bsz, nc, CHUNK, ATT_HEADS, HEAD_DIM)
    pad = ((0, 0), (LEFT_CHUNKS, 0), (0, 0), (0, 0), (0, 0))
    kp = jnp.pad(k, pad)
    vp = jnp.pad(v, pad)
    band_idx = np.arange(nc)[:, None] + np.arange(BAND_CHUNKS)[None, :]
    nk = BAND_CHUNKS * CHUNK
    kb = kp[:, band_idx].reshape(bsz, nc, nk, ATT_HEADS, HEAD_DIM)
    vb = vp[:, band_idx].reshape(bsz, nc, nk, ATT_HEADS, HEAD_DIM)
    scores = jnp.einsum('bcqhd,bckhd->bhcqk', q, kb).astype(jnp.float32) * (HEAD_DIM ** -0.5)
    key_off = (np.arange(BAND_CHUNKS)[:, None] * CHUNK - LEFT_CHUNKS * CHUNK
               + np.arange(CHUNK)[None, :]).reshape(-1)
    rel = key_off[None, :] - np.arange(CHUNK)[:, None]
    rel_idx = np.clip(rel, -MAX_REL, MAX_REL) + MAX_REL
    bias = rel_table[:, rel_idx].astype(jnp.float32)
    valid = (np.arange(nc)[:, None] - LEFT_CHUNKS + np.arange(BAND_CHUNKS)[None, :]) >= 0
    valid = np.repeat(valid, CHUNK, axis=1)
    scores = scores + bias[None, :, None]
    scores = jnp.where(valid[None, None, :, None, :], scores, NEG_INF)
    p = jax.nn.softmax(scores, axis=-1).astype(v.dtype)
    o = jnp.einsum('bhcqk,bckhd->bcqhd', p, vb)
    return o.reshape(bsz, s, D_ATT)


def setup_inputs(seed: int = 0) -> dict:
    key = jax.random.key(seed)
    ks = jax.random.split(key, 24)
    f32 = jnp.float32

    def nrm(k, shape, fan_in):
        return jax.random.normal(k, shape, f32) * (fan_in ** -0.5)

    def gain(k, shape):
        return 1.0 + 0.02 * jax.random.normal(k, shape, f32)

    u = jax.random.uniform(ks[13], (DEPTH, D_LRU), f32, 0.9, 0.999)
    sig = u ** (1.0 / LRU_C)
    lam = jnp.log(sig) - jnp.log1p(-sig)
    return {
        "x": jax.random.normal(ks[0], (BATCH, SEQ, D_MODEL), f32),
        "ffn1_norm": gain(ks[1], (DEPTH, D_MODEL)),
        "ffn1_w_gate": nrm(ks[2], (DEPTH, D_MODEL, D_FF), D_MODEL),
        "ffn1_w_up": nrm(ks[3], (DEPTH, D_MODEL, D_FF), D_MODEL),
        "ffn1_w_down": nrm(ks[4], (DEPTH, D_FF, D_MODEL), D_FF),
        "mix_norm": gain(ks[5], (DEPTH, D_MODEL)),
        "w_in": nrm(ks[6], (DEPTH, D_MODEL, D_IN), D_MODEL),
        "conv_w": nrm(ks[7], (DEPTH, CONV_WIDTH, D_LRU), CONV_WIDTH),
        "conv_b": 0.01 * jax.random.normal(ks[8], (DEPTH, D_LRU), f32),
        "lru_gate_a_w": nrm(ks[9], (DEPTH, LRU_HEADS, LRU_BLOCK, LRU_BLOCK), LRU_BLOCK),
        "lru_gate_a_b": 0.01 * jax.random.normal(ks[10], (DEPTH, D_LRU), f32),
        "lru_gate_x_w": nrm(ks[11], (DEPTH, LRU_HEADS, LRU_BLOCK, LRU_BLOCK), LRU_BLOCK),
        "lru_gate_x_b": 0.01 * jax.random.normal(ks[12], (DEPTH, D_LRU), f32),
        "lru_lambda": lam,
        "rel_bias": 0.1 * jax.random.normal(ks[14], (DEPTH, ATT_HEADS, N_REL), f32),
        "lru_out_norm": gain(ks[15], (DEPTH, D_LRU)),
        "att_out_norm": gain(ks[16], (DEPTH, D_ATT)),
        "w_out": nrm(ks[17], (DEPTH, D_MIX, D_MODEL), D_MIX),
        "ffn2_norm": gain(ks[18], (DEPTH, D_MODEL)),
        "ffn2_w_gate": nrm(ks[19], (DEPTH, D_MODEL, D_FF), D_MODEL),
        "ffn2_w_up": nrm(ks[20], (DEPTH, D_MODEL, D_FF), D_MODEL),
        "ffn2_w_down": nrm(ks[21], (DEPTH, D_FF, D_MODEL), D_FF),
        "final_norm": gain(ks[22], (D_MODEL,)),
    }


def reference(x, ffn1_norm, ffn1_w_gate, ffn1_w_up, ffn1_w_down, mix_norm, w_in,
              conv_w, conv_b, lru_gate_a_w, lru_gate_a_b, lru_gate_x_w, lru_gate_x_b,
              lru_lambda, rel_bias, lru_out_norm, att_out_norm, w_out,
              ffn2_norm, ffn2_w_gate, ffn2_w_up, ffn2_w_down, final_norm):
    splits = [D_LRU, 2 * D_LRU, 2 * D_LRU + D_ATT, 2 * D_LRU + 2 * D_ATT]
    for l in range(DEPTH):
        h = rmsnorm(x, ffn1_norm[l])
        x = x + 0.5 * swiglu(h, ffn1_w_gate[l], ffn1_w_up[l], ffn1_w_down[l])
        h = rmsnorm(x, mix_norm[l])
        z = h @ w_in[l]
        xl, gl, q, k, v = jnp.split(z, splits, axis=-1)
        xl = causal_dwconv(xl, conv_w[l], conv_b[l])
        y_lru = rg_lru(xl, lru_gate_a_w[l], lru_gate_a_b[l], lru_gate_x_w[l],
                       lru_gate_x_b[l], lru_lambda[l]) * jax.nn.gelu(gl)
        y_att = chunk_attention(q, k, v, rel_bias[l])
        y = jnp.concatenate([rmsnorm(y_lru, lru_out_norm[l]),
                             rmsnorm(y_att, att_out_norm[l])], axis=-1)
        x = x + y @ w_out[l]
        h = rmsnorm(x, ffn2_norm[l])
        x = x + 0.5 * swiglu(h, ffn2_w_gate[l], ffn2_w_up[l], ffn2_w_down[l])
    return rmsnorm(x, final_norm)
```

```python
import numpy as np
import concourse.bass as bass
import concourse.mybir as mybir
from concourse.bass_utils import run_bass_kernel_spmd

F32 = mybir.dt.float32
BF16 = mybir.dt.bfloat16
AF = mybir.ActivationFunctionType
ALU = mybir.AluOpType

D = 2048
NC_ = 16
DFF = 5632
NJ = 44
HALF = 22
DLRU = 1024
T = 512
SEQ = 2048
NCORES = 8
TOK_PER_CORE = 4096
EPS = 1e-6
SCALE = 128 ** -0.5
PADL = 256


class Sem:
    def __init__(self, handle, name):
        self.h = handle
        self.name = name
        self.count = 0


class Buf:
    __slots__ = ("name", "w", "r")

    def __init__(self, name):
        self.name = name
        self.w = None
        self.r = []


class Sched:
    ENG = ("pe", "act", "dve", "pool", "sp")

    def __init__(self):
        self.ops = {e: [] for e in self.ENG}
        self.waited = {e: {} for e in self.ENG}
        self.esem = {}
        self.pending_nosig = {e: False for e in self.ENG}

    def add(self, eng, fn, reads=(), writes=(), sig=True, dma_sem=None, extra=()):
        waits = {}

        def need(ev):
            if ev is None:
                return
            s, v = ev
            if waits.get(s, 0) < v:
                waits[s] = v

        for b in reads:
            need(b.w)
        for b in writes:
            need(b.w)
            for r in b.r:
                need(r)
        for ev in extra:
            need(ev)
        wl = []
        wd = self.waited[eng]
        own = self.esem.get(eng)
        for s, v in waits.items():
            if eng == "pe" and s is own:
                continue
            if wd.get(s, 0) >= v:
                continue
            wd[s] = v
            wl.append((s, v))
        inc = None
        if dma_sem is not None:
            dma_sem.count += 16
            ev = (dma_sem, dma_sem.count)
            inc = (dma_sem, 16)
        else:
            s = self.esem[eng]
            if sig:
                s.count += 1
                ev = (s, s.count)
                inc = (s, 1)
                self.pending_nosig[eng] = False
            else:
                ev = (s, s.count + 1)
                self.pending_nosig[eng] = True
        self.ops[eng].append((fn, wl, inc))
        for b in writes:
            b.w = ev
            b.r = []
        for b in reads:
            if b in writes:
                continue
            rs = [r for r in b.r if r[0] is not ev[0]]
            rs.append(ev)
            b.r = rs
        return ev

    def emit(self, eng, e):
        for fn, wl, inc in self.ops[eng]:
            for s, v in wl:
                e.wait_ge(s.h, v)
            ins = fn(e)
            if inc is not None:
                ins.then_inc(inc[0].h, inc[1])


def _pcol(v, nch):
    return np.ascontiguousarray(np.asarray(v, np.float32).reshape(nch, 128).T)


PC_PER_LAYER = 128
PC = {}
_o = 0
for _n, _w in (("g1", 16), ("gm", 16), ("g2", 16), ("cw", 32), ("cb", 8), ("ba", 8), ("bx", 8),
               ("lam", 8), ("ga", 8), ("gb", 8)):
    PC[_n] = (_o, _w)
    _o += _w
assert _o == PC_PER_LAYER
NPCOL = 2 * PC_PER_LAYER + 16


def pack_params(inp, depth):
    P = np.zeros((128, NPCOL), np.float32)
    for l in range(depth):
        base = l * PC_PER_LAYER

        def put(name, arr):
            o, w = PC[name]
            P[:, base + o: base + o + w] = arr

        put("g1", _pcol(inp["ffn1_norm"][l], 16))
        put("gm", _pcol(inp["mix_norm"][l], 16))
        put("g2", _pcol(inp["ffn2_norm"][l], 16))
        cw = np.concatenate([_pcol(inp["conv_w"][l][k], 8) for k in range(4)], axis=1)
        put("cw", cw)
        put("cb", _pcol(inp["conv_b"][l], 8))
        put("ba", _pcol(inp["lru_gate_a_b"][l], 8))
        put("bx", _pcol(inp["lru_gate_x_b"][l], 8))
        put("lam", _pcol(inp["lru_lambda"][l], 8))
        put("ga", _pcol(inp["lru_out_norm"][l], 8))
        put("gb", _pcol(inp["att_out_norm"][l], 8))
    P[:, 2 * PC_PER_LAYER:] = _pcol(inp["final_norm"], 16)
    return P


def build(n_tiles=8, n_layers=2, phases=("ffn1", "mix", "ffn2"), final_norm=True, scan_eng="pool", cut=99, pro=("gw", "der", "m")):
    nc = bass.Bass("TRN2", target_bir_lowering=False)
    L = n_layers
    dt_in = lambda name, shape: nc.dram_tensor(name, shape, F32, kind="ExternalInput").ap()
    xT_d = dt_in("xT", [D, TOK_PER_CORE])
    prm_d = dt_in("prm", [128, NPCOL])
    wg_d = [dt_in("w_gate1", [2, D, DFF]), dt_in("w_gate2", [2, D, DFF])]
    wu_d = [dt_in("w_up1", [2, D, DFF]), dt_in("w_up2", [2, D, DFF])]
    wd_d = [dt_in("w_down1", [2, DFF, D]), dt_in("w_down2", [2, DFF, D])]
    win_d = dt_in("w_in", [2, D, 5120])
    wout_d = dt_in("w_out", [2, D, D])
    gaw_d = dt_in("gate_a_w", [2, 16, 64, 64])
    gxw_d = dt_in("gate_x_w", [2, 16, 64, 64])
    rel_d = dt_in("rel_bias", [16, 257])
    out_d = nc.dram_tensor("outT", [D, TOK_PER_CORE], F32, kind="ExternalOutput").ap()

    def scratch(name, shape, dt=BF16):
        return nc.dram_tensor(name, shape, dt, kind="Internal").ap()

    WGU = scratch("s_wgu", [L, 2, NJ, 128, 2, NC_, 128])
    WD = scratch("s_wd", [L, 2, NC_, 2, 128, HALF, 128])
    WIN = scratch("s_win", [L, 20, 128, 2, NC_, 128])
    WOUT = scratch("s_wout", [L, 8, 128, 2, NC_, 128])
    GW = scratch("s_gw", [L, 128, 2, 8, 128])
    RS = scratch("s_r", [16, 128, 768], F32)
    MS = scratch("s_m", [128, 16, 640], F32)

    S = Sched()
    import contextlib
    es = contextlib.ExitStack()
    with es:
        def sb(name, shape, dt):
            return es.enter_context(nc.sbuf_tensor("sb_" + name, shape, dt))

        def new_sem(name):
            return Sem(es.enter_context(nc.semaphore(name)), name)

        for e in ("pe", "act", "dve", "pool"):
            S.esem[e] = new_sem("e_" + e)
        S.esem["sp"] = None

        xT = sb("xT", [128, NC_, T], F32)
        hT = sb("hT", [128, NC_, T], BF16)
        U1 = sb("U1", [128, 8224], F32)
        gg = U1[:, 0:4096].rearrange("p (c t) -> p c t", t=T)
        xl = U1[:, 4096:8224].rearrange("p (c t) -> p c t", t=T + 4)
        AT = U1[:, 0:5632].bitcast(BF16).rearrange("p (j t) -> p j t", t=T)
        qT = sb("qT", [128, 8, T], BF16)
        kTc = sb("kTc", [128, 8, T], BF16)
        Vc = sb("Vc", [128, 4, 1024], BF16)
        kTp = [sb(f"kTp{l}", [128, 8, T], BF16) for l in range(L)]
        Vp = [sb(f"Vp{l}", [128, 4, 1024], BF16) for l in range(L)]
        Mh = [sb(f"Mh{i}", [128, 640], F32) for i in range(2)]
        gwt = sb("gwt", [128, 2, 8, 128], BF16)
        NA, NB = 3, 2
        ringA = [sb(f"rA{i}", [128, 2, NC_, 128], BF16) for i in range(NA)]
        ringB = [sb(f"rB{i}", [128, HALF, 128], BF16) for i in range(NB)]
        NTMP = 6
        tmp = [sb(f"tmp{i}", [128, T], F32) for i in range(NTMP)]
        Abuf = [sb(f"Ab{i}", [128, PADL + T], F32) for i in range(2)]
        Bbuf = sb("Bb", [128, PADL + T], F32)
        ptb = [sb(f"pt{i}", [128, T], BF16) for i in range(2)]
        xcb = sb("xcb", [128, T], BF16)
        prm = sb("prm", [128, NPCOL], F32)
        der = sb("der", [128, 2, 24], F32)
        hst = sb("hst", [128, 2, 8], F32)
        ctail = sb("ctail", [128, 2, 8, 4], F32)
        ones_b = sb("ones_b", [128, 128], BF16)
        stg = [hT[:, 2 * k:2 * k + 2, :].rearrange("p a t -> p (a t)").bitcast(F32) for k in range(2)]
        psum = [es.enter_context(nc.psum_tensor(f"ps{i}", [128, T], F32)) for i in range(8)]

        B_x = [Buf(f"x{c}") for c in range(NC_)]
        B_h = [Buf(f"h{c}") for c in range(NC_)]
        B_at = [Buf(f"at{j}") for j in range(HALF)]
        B_gg = [Buf(f"gg{c}") for c in range(8)]
        B_xl = [Buf(f"xl{c}") for c in range(8)]
        B_q = [Buf(f"q{c}") for c in range(8)]
        B_kc = [Buf(f"kc{c}") for c in range(8)]
        B_vc = [Buf(f"vc{c}") for c in range(4)]
        B_kp = [[Buf(f"kp{l}_{c}") for c in range(8)] for l in range(L)]
        B_vp = [[Buf(f"vp{l}_{c}") for c in range(4)] for l in range(L)]
        B_M = [Buf("M0"), Buf("M1")]
        B_gw = Buf("gw")
        B_rA = [Buf(f"rA{i}") for i in range(NA)]
        B_rB = [Buf(f"rB{i}") for i in range(NB)]
        B_tmp = [Buf(f"tmp{i}") for i in range(NTMP)]
        B_A = [Buf("A0"), Buf("A1")]
        B_B = Buf("B")
        B_pt = [Buf("pt0"), Buf("pt1")]
        B_xcb = Buf("xcb")
        B_prm = Buf("prm")
        B_der = Buf("der")
        B_hst = [[Buf(f"hst{l}_{c}") for c in range(8)] for l in range(L)]
        B_ct = [[Buf(f"ct{l}_{c}") for c in range(8)] for l in range(L)]
        B_ones = Buf("ones")
        B_stg = [Buf("stg0"), Buf("stg1")]
        B_ps = [Buf(f"ps{i}") for i in range(8)]
        B_out = Buf("out")
        B_cv = {}

        sem_rA = [new_sem(f"rA{i}") for i in range(NA)]
        sem_rB = [new_sem(f"rB{i}") for i in range(NB)]
        sem_M = [new_sem("M0"), new_sem("M1")]
        sem_gw = new_sem("gw")
        sem_x = [new_sem(f"xld{i}") for i in range(4)]
        sem_out = [new_sem("ost0"), new_sem("ost1")]
        _mc = [0]

        def misc_sem():
            _mc[0] += 1
            return new_sem(f"misc{_mc[0]}")
        sem_prm = new_sem("prm")

        def col(l, name, i=0):
            o, _ = PC[name]
            k = l * PC_PER_LAYER + o + i
            return prm[:, k:k + 1]

        def alias_sync(srcs, dsts):
            evs = []
            for sbf in srcs:
                if sbf.w is not None:
                    evs.append(sbf.w)
                evs.extend(sbf.r)
            for d_ in dsts:
                d_.r = list(d_.r) + evs

        S.add("sp", lambda e: e.dma_start(out=prm[:], in_=prm_d), writes=[B_prm], dma_sem=sem_prm)
        S.add("dve", lambda e: e.memset(ones_b[:], 1.0), writes=[B_ones])
        S.add("dve", lambda e: e.memset(hst[:], 0.0), writes=[b for l in range(L) for b in B_hst[l]])
        S.add("dve", lambda e: e.memset(ctail[:], 0.0), writes=[b for l in range(L) for b in B_ct[l]])
        for i in range(2):
            S.add("dve", (lambda i: lambda e: e.memset(Abuf[i][:, 0:PADL], 1.0))(i), writes=[B_A[i]])
        S.add("dve", lambda e: e.memset(Bbuf[:, 0:PADL], 0.0), writes=[B_B])

        def conv_group(key, dmas):
            sem = new_sem("cv_" + key)
            b = Buf("cv_" + key)
            for fn in dmas:
                S.add("pool", fn, writes=[], dma_sem=sem)
            b.w = (sem, sem.count)
            B_cv[key] = b

        def cv(dst, src):
            return lambda e: e.dma_start(out=dst, in_=src)

        def conv_ffn(l, f):
            gsrc = wg_d[f][l].rearrange("(c p) (j q) -> j p c q", p=128, q=128)
            usrc = wu_d[f][l].rearrange("(c p) (j q) -> j p c q", p=128, q=128)
            dm = []
            for j0 in range(NJ):
                dm.append(cv(WGU[l, f, j0, :, 0], gsrc[j0]))
                dm.append(cv(WGU[l, f, j0, :, 1], usrc[j0]))
            conv_group(f"gu{l}{f}", dm)
            dsrc = wd_d[f][l].rearrange("(h jj p) (c q) -> c h p jj q", h=2, p=128, q=128)
            dm = []
            for c in range(NC_):
                for h in range(2):
                    dm.append(cv(WD[l, f, c, h], dsrc[c, h]))
            conv_group(f"d{l}{f}", dm)

        for l in range(L):
            for ph in phases:
                if ph == "ffn1":
                    conv_ffn(l, 0)
                elif ph == "ffn2":
                    conv_ffn(l, 1)
                else:
                    isrc = win_d[l].rearrange("(c p) (jp two q) -> jp p two c q", p=128, two=2, q=128)
                    dm = [cv(WIN[l, j0, :, tw], isrc[j0, :, tw]) for j0 in range(20) for tw in range(2)]
                    conv_group(f"in{l}", dm)
                    osrc = wout_d[l].rearrange("(c p) (jp two q) -> jp p two c q", p=128, two=2, q=128)
                    dm = [cv(WOUT[l, j0, :, tw], osrc[j0, :, tw]) for j0 in range(8) for tw in range(2)]
                    conv_group(f"out{l}", dm)

        if "mix" in phases:
            S.add("dve", lambda e: e.memset(gwt[:], 0.0), writes=[B_gw])
            B_gws = Buf("gws")
            for l in (range(L) if "gw" in pro else []):
                S.add("sp", (lambda l: lambda e: e.dma_start(out=GW[l], in_=gwt[:]))(l), reads=[B_gw],
                      writes=[B_gws], dma_sem=misc_sem())
            semg = new_sem("cv_gw")
            for l in (range(L) if "gw" in pro else []):
                for g, src in ((0, gaw_d), (1, gxw_d)):
                    sr = src[l].rearrange("(cc e) i j -> e i cc j", e=2)
                    for e_ in range(2):
                        dst = GW[l, e_ * 64:(e_ + 1) * 64, g, :, e_ * 64:(e_ + 1) * 64]
                        S.add("pool", cv(dst, sr[e_]), reads=[], writes=[], dma_sem=semg, extra=[B_gws.w])
            B_gws.w = (semg, semg.count)

            for l in (range(L) if "der" in pro else []):
                lam = prm[:, l * PC_PER_LAYER + PC["lam"][0]: l * PC_PER_LAYER + PC["lam"][0] + 8]
                t0, t1, t2, t3 = (tmp[i][:, 0:8] for i in range(4))
                bt = [B_tmp[i] for i in range(4)]
                dd = der[:, l, 0:8]
                S.add("dve", lambda e, lam=lam, t0=t0: e.tensor_scalar(out=t0, in0=lam, scalar1=-1.0, scalar2=None, op0=ALU.mult),
                      reads=[B_prm], writes=[bt[0]])
                S.add("dve", lambda e, lam=lam, t0=t0: e.tensor_tensor(out=t0, in0=t0, in1=lam, op=ALU.max),
                      reads=[B_prm, bt[0]], writes=[bt[0]])
                S.add("act", lambda e, t0=t0, t1=t1: e.activation(out=t1, in_=t0, func=AF.Exp, scale=-1.0),
                      reads=[bt[0]], writes=[bt[1]])
                S.add("dve", lambda e, t1=t1, t2=t2: e.tensor_scalar(out=t2, in0=t1, scalar1=2.0, scalar2=None, op0=ALU.add),
                      reads=[bt[1]], writes=[bt[2]])
                S.add("dve", lambda e, t2=t2: e.reciprocal(out=t2, in_=t2), reads=[bt[2]], writes=[bt[2]])
                S.add("dve", lambda e, t1=t1, t2=t2: e.tensor_tensor(out=t2, in0=t2, in1=t1, op=ALU.mult),
                      reads=[bt[1], bt[2]], writes=[bt[2]])
                S.add("dve", lambda e, t2=t2, t3=t3: e.tensor_tensor(out=t3, in0=t2, in1=t2, op=ALU.mult),
                      reads=[bt[2]], writes=[bt[3]])
                S.add("dve", lambda e, t3=t3, t1=t1: e.tensor_scalar(out=t1, in0=t3, scalar1=1.0 / 11, scalar2=1.0 / 9, op0=ALU.mult, op1=ALU.add),
                      reads=[bt[3]], writes=[bt[1]])
                for cst in (1.0 / 7, 1.0 / 5, 1.0 / 3, 1.0):
                    S.add("dve", lambda e, t3=t3, t1=t1: e.tensor_tensor(out=t1, in0=t1, in1=t3, op=ALU.mult),
                          reads=[bt[1], bt[3]], writes=[bt[1]])
                    S.add("dve", lambda e, t1=t1, cst=cst: e.tensor_scalar(out=t1, in0=t1, scalar1=cst, scalar2=None, op0=ALU.add),
                          reads=[bt[1]], writes=[bt[1]])
                S.add("dve", lambda e, t1=t1, t2=t2: e.tensor_tensor(out=t1, in0=t1, in1=t2, op=ALU.mult),
                      reads=[bt[1], bt[2]], writes=[bt[1]])
                S.add("dve", lambda e, lam=lam, t0=t0: e.tensor_scalar(out=t0, in0=lam, scalar1=-1.0, scalar2=0.0, op0=ALU.mult, op1=ALU.max),
                      reads=[B_prm], writes=[bt[0]])
                S.add("dve", lambda e, t1=t1: e.tensor_scalar(out=t1, in0=t1, scalar1=-16.0, scalar2=None, op0=ALU.mult),
                      reads=[bt[1]], writes=[bt[1]])
                S.add("dve", lambda e, t0=t0, t1=t1, dd=dd: e.scalar_tensor_tensor(out=dd, in0=t0, scalar=-8.0, in1=t1, op0=ALU.mult, op1=ALU.add),
                      reads=[bt[0], bt[1]], writes=[B_der])
                for nm, o in (("ba", 8), ("bx", 16)):
                    src = prm[:, l * PC_PER_LAYER + PC[nm][0]: l * PC_PER_LAYER + PC[nm][0] + 8]
                    S.add("dve", lambda e, src=src, o=o, l=l: e.tensor_scalar(out=der[:, l, o:o + 8], in0=src, scalar1=-1.0, scalar2=None, op0=ALU.mult),
                          reads=[B_prm, B_der], writes=[B_der])

            tbflat = U1[:, 4096:4096 + 16 * 257]
            ext4 = U1[:, 0:3072].rearrange("p (a j) -> p a j", j=768)
            mrb = hT[:, 0:10, :].rearrange("p a t -> p (a t)").bitcast(F32).rearrange("p (a u) -> p a u", u=640)
            B_tb, B_ext, B_mrb, B_rs, B_ms = Buf("tb"), Buf("ext"), Buf("mrb"), Buf("rs"), Buf("ms")
            S.add("sp", lambda e: e.dma_start(out=tbflat, in_=bass.AP(rel_d.tensor, 0, [[0, 128], [1, 16 * 257]])),
                  writes=[B_tb], dma_sem=misc_sem())
            for g4 in (range(4) if "m" in pro else []):
                for i in range(4):
                    lh = g4 * 4 + i
                    rev = bass.AP(tbflat.tensor, tbflat[:, lh * 257 + 255:lh * 257 + 256].offset, [list(tbflat.ap[0]), [-1, 256]])
                    S.add("dve", lambda e, i=i, rev=rev: e.tensor_copy(out=ext4[:, i, 0:256], in_=rev), reads=[B_tb], writes=[B_ext])
                    t0c = tbflat[:, lh * 257:lh * 257 + 1]
                    S.add("dve", lambda e, i=i, t0c=t0c: e.tensor_scalar(out=ext4[:, i, 256:768], in0=tbflat[:, 0:512], scalar1=0.0, scalar2=t0c, op0=ALU.mult, op1=ALU.add),
                          reads=[B_tb, B_ext], writes=[B_ext])
                S.add("act", lambda e: e.activation(out=ext4, in_=ext4, func=AF.Exp), reads=[B_ext], writes=[B_ext])
                S.add("sp", lambda e, g4=g4: e.dma_start(out=RS[g4 * 4:(g4 + 1) * 4].rearrange("a p j -> p a j"), in_=ext4),
                      reads=[B_ext], writes=[B_rs], dma_sem=misc_sem())
                srcm = bass.AP(RS.tensor, g4 * 4 * 128 * 768 + 127, [[767, 128], [128 * 768, 4], [1, 640]])
                S.add("sp", lambda e, srcm=srcm: e.dma_start(out=mrb, in_=srcm), reads=[B_rs], writes=[B_mrb], dma_sem=misc_sem())
                S.add("dve", lambda e: e.memset(mrb[0:64, :, 576:640], 0.0), reads=[B_mrb], writes=[B_mrb])
                S.add("dve", lambda e: e.memset(mrb[64:128, :, 0:64], 0.0), reads=[B_mrb], writes=[B_mrb])
                S.add("sp", lambda e, g4=g4: e.dma_start(out=MS[:, g4 * 4:(g4 + 1) * 4, :], in_=mrb), reads=[B_mrb], writes=[B_ms], dma_sem=misc_sem())
            B_ms_all = Buf("ms_all")
            B_ms_all.w = B_ms.w
            alias_sync([B_tb, B_ext, B_mrb], B_at + B_gg + B_xl + B_h)

        def tile_info(ti):
            return ti // 4, ti % 4

        planA, planB = [], []
        for ti in range(n_tiles):
            for l in range(L):
                for ph in phases:
                    if ph in ("ffn1", "ffn2"):
                        f = 0 if ph == "ffn1" else 1
                        for h in range(2):
                            for jj in range(HALF):
                                planA.append(("gu", l, f, h * HALF + jj))
                            for c in range(NC_):
                                planB.append((l, f, c, h))
                    else:
                        for jp in range(20):
                            planA.append(("in", l, jp))
                        for jp in range(8):
                            planA.append(("out", l, jp))
        stA = {"next_load": 0, "next_use": 0}
        stB = {"next_load": 0, "next_use": 0}

        def loadA():
            k = stA["next_load"]
            if k >= len(planA):
                return
            stA["next_load"] += 1
            u = planA[k]
            slot = k % NA
            if u[0] == "gu":
                src = WGU[u[1], u[2], u[3]]
                cvb = B_cv[f"gu{u[1]}{u[2]}"]
            elif u[0] == "in":
                src = WIN[u[1], u[2]]
                cvb = B_cv[f"in{u[1]}"]
            else:
                src = WOUT[u[1], u[2]]
                cvb = B_cv[f"out{u[1]}"]
            S.add("sp", lambda e, slot=slot, src=src: e.dma_start(out=ringA[slot][:], in_=src),
                  writes=[B_rA[slot]], dma_sem=sem_rA[slot], extra=[cvb.w])

        def loadB():
            k = stB["next_load"]
            if k >= len(planB):
                return
            stB["next_load"] += 1
            l, f, c, h = planB[k]
            slot = k % NB
            src = WD[l, f, c, h]
            S.add("sp", lambda e, slot=slot, src=src: e.dma_start(out=ringB[slot][:], in_=src),
                  writes=[B_rB[slot]], dma_sem=sem_rB[slot], extra=[B_cv[f"d{l}{f}"].w])

        def useA(expect):
            k = stA["next_use"]
            assert planA[k] == expect, (planA[k], expect)
            stA["next_use"] += 1
            return k % NA

        def useB(expect):
            k = stB["next_use"]
            assert planB[k] == expect, (planB[k], expect)
            stB["next_use"] += 1
            return k % NB

        for _ in range(NA):
            loadA()
        for _ in range(NB):
            loadB()

        tcnt = {"t": 0, "ps": 0}

        def tmp_next():
            i = tcnt["t"] % NTMP
            tcnt["t"] += 1
            return tmp[i], B_tmp[i]

        def rms_stats(srcs, nd, ps_i):
            n = len(srcs)
            for i, (ap, b) in enumerate(srcs):
                k = i % 2
                S.add("dve", lambda e, ap=ap, k=k: e.tensor_tensor(out=ptb[k][:], in0=ap, in1=ap, op=ALU.mult),
                      reads=[b], writes=[B_pt[k]])
                S.add("pe", lambda e, k=k, i=i, n=n: e.matmul(psum[ps_i][:], lhsT=ones_b[:], rhs=ptb[k][:], start=(i == 0), stop=(i == n - 1)),
                      reads=[B_pt[k], B_ones], writes=[B_ps[ps_i]], sig=True)
            r, rb = tmp_next()
            S.add("act", lambda e, r=r: e.activation(out=r[:], in_=psum[ps_i][:], func=AF.Ln, scale=1.0 / nd, bias=EPS),
                  reads=[B_ps[ps_i]], writes=[rb])
            S.add("act", lambda e, r=r: e.activation(out=r[:], in_=r[:], func=AF.Exp, scale=-0.5), reads=[rb], writes=[rb])
            return r, rb

        def norm_to_h(l, gname, gbase=None):
            r, rb = rms_stats([(xT[:, c, :], B_x[c]) for c in range(NC_)], float(D), 6)
            for c in range(NC_):
                g = col(l, gname, c) if gbase is None else prm[:, gbase + c: gbase + c + 1]
                S.add("dve", lambda e, c=c, g=g, r=r: e.scalar_tensor_tensor(out=hT[:, c, :], in0=xT[:, c, :], scalar=g, in1=r[:], op0=ALU.mult, op1=ALU.mult),
                      reads=[B_x[c], rb, B_prm], writes=[B_h[c]])

        def ffn(l, f):
            gname = "g1" if f == 0 else "g2"
            alias_sync(B_gg + B_xl, B_at)
            norm_to_h(l, gname)
            for h in range(2):
                for jj in range(HALF):
                    j = h * HALF + jj
                    slot = useA(("gu", l, f, j))
                    pg, pu = j % 2, 2 + j % 2
                    for which, pi in ((0, pg), (1, pu)):
                        for c in range(NC_):
                            S.add("pe", lambda e, slot=slot, which=which, c=c, pi=pi: e.matmul(
                                psum[pi][:], lhsT=ringA[slot][:, which, c, :], rhs=hT[:, c, :], start=(c == 0), stop=(c == NC_ - 1)),
                                reads=[B_rA[slot], B_h[c]], writes=[B_ps[pi]], sig=(c == NC_ - 1))
                    loadA()
                    sg, sgb = tmp_next()
                    S.add("act", lambda e, sg=sg, pg=pg: e.activation(out=sg[:], in_=psum[pg][:], func=AF.Silu),
                          reads=[B_ps[pg]], writes=[sgb])
                    S.add("dve", lambda e, sg=sg, pu=pu, jj=jj: e.tensor_tensor(out=AT[:, jj, :], in0=sg[:], in1=psum[pu][:], op=ALU.mult),
                          reads=[sgb, B_ps[pu]], writes=[B_at[jj]])
                for c in range(NC_):
                    slot = useB((l, f, c, h))
                    py = 4 + c % 2
                    for jj in range(HALF):
                        S.add("pe", lambda e, slot=slot, jj=jj, py=py: e.matmul(
                            psum[py][:], lhsT=ringB[slot][:, jj, :], rhs=AT[:, jj, :], start=(jj == 0), stop=(jj == HALF - 1)),
                            reads=[B_rB[slot], B_at[jj]], writes=[B_ps[py]], sig=(jj == HALF - 1))
                    loadB()
                    S.add("dve", lambda e, c=c, py=py: e.scalar_tensor_tensor(out=xT[:, c, :], in0=psum[py][:], scalar=0.5, in1=xT[:, c, :], op0=ALU.mult, op1=ALU.add),
                          reads=[B_ps[py], B_x[c]], writes=[B_x[c]])

        def mixer(l, first_in_seq):
            alias_sync(B_at, B_gg + B_xl)
            norm_to_h(l, "gm")
            S.add("sp", lambda e: e.dma_start(out=gwt[:], in_=GW[l]), writes=[B_gw], dma_sem=sem_gw, extra=[B_gws.w])

            def win_group(jp, which, pi, kind, idx):
                slot = slotc[0]
                for c in range(NC_):
                    S.add("pe", lambda e, slot=slot, which=which, c=c, pi=pi: e.matmul(
                        psum[pi][:], lhsT=ringA[slot][:, which, c, :], rhs=hT[:, c, :], start=(c == 0), stop=(c == NC_ - 1)),
                        reads=[B_rA[slot], B_h[c]], writes=[B_ps[pi]], sig=(c == NC_ - 1))

            slotc = [0]
            for jp in range(16):
                slotc[0] = useA(("in", l, jp))
                for which in range(2):
                    jcol = jp * 2 + which
                    pi = jcol % 2
                    win_group(jp, which, pi, None, None)
                    kind, idx = jcol // 8, jcol % 8
                    if kind == 0:
                        S.add("act", lambda e, idx=idx, pi=pi: e.activation(out=xl[:, idx, 3:3 + T], in_=psum[pi][:], func=AF.Copy),
                              reads=[B_ps[pi]], writes=[B_xl[idx]])
                        S.add("pool", lambda e, idx=idx: e.tensor_copy(out=xl[:, idx, 0:3], in_=ctail[:, l, idx, 0:3]),
                              reads=[B_ct[l][idx]], writes=[B_xl[idx]])
                    elif kind == 1:
                        g = gg[:, idx, :]
                        t1, t1b = tmp_next()
                        S.add("act", lambda e, g=g, pi=pi: e.activation(out=g, in_=psum[pi][:], func=AF.Copy),
                              reads=[B_ps[pi]], writes=[B_gg[idx]])
                        S.add("dve", lambda e, g=g, t1=t1: e.tensor_tensor(out=t1[:], in0=g, in1=g, op=ALU.mult),
                              reads=[B_gg[idx]], writes=[t1b])
                        S.add("dve", lambda e, t1=t1: e.tensor_scalar(out=t1[:], in0=t1[:], scalar1=0.044715, scalar2=1.0, op0=ALU.mult, op1=ALU.add),
                              reads=[t1b], writes=[t1b])
                        S.add("dve", lambda e, g=g, t1=t1: e.tensor_tensor(out=t1[:], in0=t1[:], in1=g, op=ALU.mult),
                              reads=[t1b, B_gg[idx]], writes=[t1b])
                        S.add("act", lambda e, t1=t1: e.activation(out=t1[:], in_=t1[:], func=AF.Exp, scale=-1.5957691216057308),
                              reads=[t1b], writes=[t1b])
                        S.add("dve", lambda e, t1=t1: e.tensor_scalar(out=t1[:], in0=t1[:], scalar1=1.0, scalar2=None, op0=ALU.add),
                              reads=[t1b], writes=[t1b])
                        S.add("dve", lambda e, t1=t1: e.reciprocal(out=t1[:], in_=t1[:]), reads=[t1b], writes=[t1b])
                        S.add("dve", lambda e, g=g, t1=t1: e.tensor_tensor(out=g, in0=g, in1=t1[:], op=ALU.mult),
                              reads=[t1b, B_gg[idx]], writes=[B_gg[idx]])
                    elif kind == 2:
                        S.add("act", lambda e, idx=idx, pi=pi: e.activation(out=qT[:, idx, :], in_=psum[pi][:], func=AF.Copy),
                              reads=[B_ps[pi]], writes=[B_q[idx]])
                    else:
                        S.add("dve", lambda e, idx=idx, pi=pi: e.tensor_copy(out=kTc[:, idx, :], in_=psum[pi][:]),
                              reads=[B_ps[pi]], writes=[B_kc[idx]])
                loadA()
            if cut < 2:
                return
            for jq in range(2):
                for jpp in range(2):
                    jp = 16 + jq * 2 + jpp
                    slot = useA(("in", l, jp))
                    for which in range(2):
                        jc = jpp * 2 + which
                        for tb in range(4):
                            pi = 2 + tb
                            for c in range(NC_):
                                S.add("pe", lambda e, slot=slot, which=which, c=c, pi=pi, jc=jc, tb=tb: e.matmul(
                                    psum[pi][:, jc * 128:(jc + 1) * 128], lhsT=hT[:, c, tb * 128:(tb + 1) * 128], rhs=ringA[slot][:, which, c, :],
                                    start=(c == 0), stop=(c == NC_ - 1)),
                                    reads=[B_rA[slot], B_h[c]], writes=[B_ps[pi]], sig=(c == NC_ - 1))
                    loadA()
                for tb in range(4):
                    pi = 2 + tb
                    eng = "act" if tb % 2 == 0 else "dve"
                    if eng == "act":
                        S.add("act", lambda e, tb=tb, pi=pi, jq=jq: e.activation(out=Vc[:, tb, jq * 512:(jq + 1) * 512], in_=psum[pi][:], func=AF.Copy),
                              reads=[B_ps[pi]], writes=[B_vc[tb]])
                    else:
                        S.add("dve", lambda e, tb=tb, pi=pi, jq=jq: e.tensor_copy(out=Vc[:, tb, jq * 512:(jq + 1) * 512], in_=psum[pi][:]),
                              reads=[B_ps[pi]], writes=[B_vc[tb]])

            if cut < 3:
                return
            SE = scan_eng
            for cc in range(8):
                xc, xcbuf = tmp_next()
                cw = [col(l, "cw", k * 8 + cc) for k in range(4)]
                S.add("dve", lambda e, cc=cc, xc=xc, cw=cw: e.tensor_scalar(out=xc[:], in0=xl[:, cc, 0:T], scalar1=cw[0], scalar2=col(l, "cb", cc), op0=ALU.mult, op1=ALU.add),
                      reads=[B_xl[cc], B_prm], writes=[xcbuf])
                for k in (1, 2, 3):
                    S.add("dve", lambda e, cc=cc, xc=xc, cw=cw, k=k: e.scalar_tensor_tensor(out=xc[:], in0=xl[:, cc, k:k + T], scalar=cw[k], in1=xc[:], op0=ALU.mult, op1=ALU.add),
                          reads=[B_xl[cc], xcbuf, B_prm], writes=[xcbuf])
                S.add("pool", lambda e, cc=cc: e.tensor_copy(out=ctail[:, l, cc, 0:3], in_=xl[:, cc, T:T + 3]),
                      reads=[B_xl[cc]], writes=[B_ct[l][cc]])
                S.add("act", lambda e, xc=xc: e.activation(out=xcb[:], in_=xc[:], func=AF.Copy), reads=[xcbuf], writes=[B_xcb])
                for g in range(2):
                    S.add("pe", lambda e, g=g, cc=cc: e.matmul(psum[g][:], lhsT=gwt[:, g, cc, :], rhs=xcb[:], start=True, stop=True),
                          reads=[B_gw, B_xcb], writes=[B_ps[g]])
                ra, rab = tmp_next()
                ri, rib = tmp_next()
                for g, (rt, rtb) in enumerate(((ra, rab), (ri, rib))):
                    nb = der[:, l, 8 + 8 * g + cc: 8 + 8 * g + cc + 1]
                    S.add("act", lambda e, g=g, rt=rt, nb=nb: e.activation(out=rt[:], in_=psum[g][:], func=AF.Exp, scale=-1.0, bias=nb),
                          reads=[B_ps[g], B_der], writes=[rtb])
                    S.add("dve", lambda e, rt=rt: e.tensor_scalar(out=rt[:], in0=rt[:], scalar1=1.0, scalar2=None, op0=ALU.add), reads=[rtb], writes=[rtb])
                    S.add("dve", lambda e, rt=rt: e.reciprocal(out=rt[:], in_=rt[:]), reads=[rtb], writes=[rtb])
                S.add("act", lambda e, ra=ra, cc=cc: e.activation(out=Abuf[0][:, PADL:], in_=ra[:], func=AF.Exp, scale=der[:, l, cc:cc + 1]),
                      reads=[rab, B_der], writes=[B_A[0]])
                S.add("dve", lambda e, ra=ra: e.tensor_tensor(out=ra[:], in0=Abuf[0][:, PADL:], in1=Abuf[0][:, PADL:], op=ALU.mult),
                      reads=[B_A[0]], writes=[rab])
                S.add("dve", lambda e, ra=ra: e.tensor_scalar(out=ra[:], in0=ra[:], scalar1=-1.0, scalar2=1.0, op0=ALU.mult, op1=ALU.add), reads=[rab], writes=[rab])
                S.add("dve", lambda e, ra=ra: e.tensor_scalar(out=ra[:], in0=ra[:], scalar1=1e-30, scalar2=None, op0=ALU.max), reads=[rab], writes=[rab])
                S.add("act", lambda e, ra=ra: e.activation(out=ra[:], in_=ra[:], func=AF.Ln), reads=[rab], writes=[rab])
                S.add("act", lambda e, ra=ra: e.activation(out=ra[:], in_=ra[:], func=AF.Exp, scale=0.5), reads=[rab], writes=[rab])
                S.add("dve", lambda e, ra=ra, ri=ri: e.tensor_tensor(out=ra[:], in0=ra[:], in1=ri[:], op=ALU.mult), reads=[rab, rib], writes=[rab])
                S.add("dve", lambda e, ra=ra, xc=xc: e.tensor_tensor(out=Bbuf[:, PADL:], in0=ra[:], in1=xc[:], op=ALU.mult),
                      reads=[rab, xcbuf], writes=[B_B])
                S.add("dve", lambda e, cc=cc: e.scalar_tensor_tensor(out=Bbuf[:, PADL:PADL + 1], in0=Abuf[0][:, PADL:PADL + 1], scalar=hst[:, l, cc:cc + 1],
                                                                     in1=Bbuf[:, PADL:PADL + 1], op0=ALU.mult, op1=ALU.add),
                      reads=[B_A[0], B_B, B_hst[l][cc]], writes=[B_B])
                st, stb = tmp_next()
                cur = 0
                dsteps = [1, 2, 4, 8, 16, 32, 64, 128, 256]
                for si, dd_ in enumerate(dsteps):
                    S.add(SE, lambda e, cur=cur, dd_=dd_, st=st: e.tensor_tensor(out=st[:], in0=Abuf[cur][:, PADL:], in1=Bbuf[:, PADL - dd_:PADL + T - dd_], op=ALU.mult),
                          reads=[B_A[cur], B_B], writes=[stb])
                    S.add(SE, lambda e, st=st: e.tensor_tensor(out=Bbuf[:, PADL:], in0=Bbuf[:, PADL:], in1=st[:], op=ALU.add),
                          reads=[stb, B_B], writes=[B_B])
                    if si < len(dsteps) - 1:
                        S.add(SE, lambda e, cur=cur, dd_=dd_: e.tensor_tensor(out=Abuf[1 - cur][:, PADL:], in0=Abuf[cur][:, PADL:], in1=Abuf[cur][:, PADL - dd_:PADL + T - dd_], op=ALU.mult),
                              reads=[B_A[cur]], writes=[B_A[1 - cur]])
                        cur = 1 - cur
                if cur != 0:
                    pass
                S.add("pool", lambda e, cc=cc: e.tensor_copy(out=hst[:, l, cc:cc + 1], in_=Bbuf[:, PADL + T - 1:PADL + T]),
                      reads=[B_B], writes=[B_hst[l][cc]])
                S.add("dve", lambda e, cc=cc: e.tensor_tensor(out=gg[:, cc, :], in0=gg[:, cc, :], in1=Bbuf[:, PADL:], op=ALU.mult),
                      reads=[B_B, B_gg[cc]], writes=[B_gg[cc]])

            if cut < 4:
                return
            blocks = [4, 5, 6, 7] + ([] if first_in_seq else [3, 2, 1, 0])
            seq = [(h, b) for h in range(8) for b in blocks]

            def kblock(h, b):
                if b >= 4:
                    return kTc[:, h, (b - 4) * 128:(b - 3) * 128], B_kc[h]
                return kTp[l][:, h, b * 128:(b + 1) * 128], B_kp[l][h]

            def vblock(h, b):
                if b >= 4:
                    return Vc[:, b - 4, h * 128:(h + 1) * 128], B_vc[b - 4]
                return Vp[l][:, b, h * 128:(h + 1) * 128], B_vp[l][b]

            def qrange(b):
                return max(0, 128 * b - 512), min(512, 128 * b + 128)

            def emit_S(k):
                h, b = seq[k]
                qs, qe = qrange(b)
                kap, kb = kblock(h, b)
                pi = k % 2
                S.add("pe", lambda e, kap=kap, h=h, qs=qs, qe=qe, pi=pi: e.matmul(psum[pi][:, 0:qe - qs], lhsT=kap, rhs=qT[:, h, qs:qe], start=True, stop=True),
                      reads=[kb, B_q[h]], writes=[B_ps[pi]])

            def emit_PV(k):
                h, b = seq[k]
                qs, qe = qrange(b)
                n = qe - qs
                pi = k % 2
                us = qs - 128 * b + 512
                mi = h % 2
                et, etb = tmp_next()
                S.add("act", lambda e, et=et, pi=pi, n=n: e.activation(out=et[:, 0:n], in_=psum[pi][:, 0:n], func=AF.Exp, scale=SCALE),
                      reads=[B_ps[pi]], writes=[etb])
                S.add("dve", lambda e, et=et, pi=pi, n=n, us=us, mi=mi: e.tensor_tensor(out=ptb[pi][:, 0:n], in0=et[:, 0:n], in1=Mh[mi][:, us:us + n], op=ALU.mult),
                      reads=[etb, B_M[mi]], writes=[B_pt[pi]])
                vap, vb = vblock(h, b)
                po, pd = 2 + h % 2, 4 + h % 2
                first = (b == blocks[0])
                last = (b == blocks[-1])
                S.add("pe", lambda e, vap=vap, pi=pi, n=n, qs=qs, qe=qe, po=po, first=first, last=last: e.matmul(
                    psum[po][:, qs:qe], lhsT=vap, rhs=ptb[pi][:, 0:n], start=first, stop=last),
                    reads=[vb, B_pt[pi]], writes=[B_ps[po]], sig=False)
                S.add("pe", lambda e, pi=pi, n=n, qs=qs, qe=qe, pd=pd, first=first, last=last: e.matmul(
                    psum[pd][:, qs:qe], lhsT=ones_b[:], rhs=ptb[pi][:, 0:n], start=first, stop=last),
                    reads=[B_ones, B_pt[pi]], writes=[B_ps[pd]], sig=True)
                if last:
                    rd, rdb = tmp_next()
                    S.add("dve", lambda e, rd=rd, pd=pd: e.reciprocal(out=rd[:], in_=psum[pd][:]), reads=[B_ps[pd]], writes=[rdb])
                    S.add("dve", lambda e, rd=rd, po=po, h=h: e.tensor_tensor(out=xl[:, h, 0:T], in0=rd[:], in1=psum[po][:], op=ALU.mult),
                          reads=[rdb, B_ps[po]], writes=[B_xl[h]])

            def load_M(h):
                mi = h % 2
                S.add("sp", lambda e, h=h, mi=mi: e.dma_start(out=Mh[mi][:], in_=MS[:, l * 8 + h, :]), writes=[B_M[mi]], dma_sem=sem_M[mi],
                      extra=[B_ms_all.w])

            load_M(0)
            load_M(1)
            nb_ = len(blocks)
            emit_S(0)
            for k in range(len(seq)):
                if k + 1 < len(seq):
                    emit_S(k + 1)
                emit_PV(k)
                h, b = seq[k]
                if b == blocks[-1] and h + 2 < 8:
                    load_M(h + 2)
            if cut < 5:
                return
            for h in range(8):
                S.add("pool", lambda e, h=h: e.tensor_copy(out=kTp[l][:, h, :], in_=kTc[:, h, :]), reads=[B_kc[h]], writes=[B_kp[l][h]])
            for tb in range(4):
                S.add("pool", lambda e, tb=tb: e.tensor_copy(out=Vp[l][:, tb, :], in_=Vc[:, tb, :]), reads=[B_vc[tb]], writes=[B_vp[l][tb]])

            if cut < 6:
                return
            rA, rAb = rms_stats([(gg[:, c, :], B_gg[c]) for c in range(8)], float(DLRU), 6)
            rB, rBb = rms_stats([(xl[:, c, 0:T], B_xl[c]) for c in range(8)], float(DLRU), 7)
            for c in range(8):
                S.add("dve", lambda e, c=c: e.scalar_tensor_tensor(out=hT[:, c, :], in0=gg[:, c, :], scalar=col(l, "ga", c), in1=rA[:], op0=ALU.mult, op1=ALU.mult),
                      reads=[B_gg[c], rAb, B_prm], writes=[B_h[c]])
            for c in range(8):
                S.add("dve", lambda e, c=c: e.scalar_tensor_tensor(out=hT[:, 8 + c, :], in0=xl[:, c, 0:T], scalar=col(l, "gb", c), in1=rB[:], op0=ALU.mult, op1=ALU.mult),
                      reads=[B_xl[c], rBb, B_prm], writes=[B_h[8 + c]])
            for jp in range(8):
                slot = useA(("out", l, jp))
                for which in range(2):
                    dc = jp * 2 + which
                    pi = dc % 2
                    for c in range(NC_):
                        S.add("pe", lambda e, slot=slot, which=which, c=c, pi=pi: e.matmul(
                            psum[pi][:], lhsT=ringA[slot][:, which, c, :], rhs=hT[:, c, :], start=(c == 0), stop=(c == NC_ - 1)),
                            reads=[B_rA[slot], B_h[c]], writes=[B_ps[pi]], sig=(c == NC_ - 1))
                    S.add("dve", lambda e, dc=dc, pi=pi: e.tensor_tensor(out=xT[:, dc, :], in0=xT[:, dc, :], in1=psum[pi][:], op=ALU.add),
                          reads=[B_ps[pi], B_x[dc]], writes=[B_x[dc]])
                loadA()

        for ti in range(n_tiles):
            sq, pos = tile_info(ti)
            tok0 = ti * T
            src = xT_d.rearrange("(c p) t -> p c t", p=128)
            for c4 in range(4):
                S.add("sp", lambda e, c4=c4, tok0=tok0: e.dma_start(out=xT[:, c4 * 4:(c4 + 1) * 4, :], in_=src[:, c4 * 4:(c4 + 1) * 4, tok0:tok0 + T]),
                      writes=B_x[c4 * 4:(c4 + 1) * 4], dma_sem=sem_x[c4])
            if pos == 0 and "mix" in phases and ti > 0:
                S.add("dve", lambda e: e.memset(hst[:], 0.0), writes=[b for l in range(L) for b in B_hst[l]])
                S.add("dve", lambda e: e.memset(ctail[:], 0.0), writes=[b for l in range(L) for b in B_ct[l]])
            for l in range(L):
                for ph in phases:
                    if ph == "ffn1":
                        ffn(l, 0)
                    elif ph == "ffn2":
                        ffn(l, 1)
                    else:
                        mixer(l, pos == 0)
            if final_norm:
                r, rb = rms_stats([(xT[:, c, :], B_x[c]) for c in range(NC_)], float(D), 6)
            alias_sync(B_h[0:4], B_stg)
            dst = out_d.rearrange("(c p) t -> p c t", p=128)
            for c in range(NC_):
                k = c % 2
                if final_norm:
                    g = prm[:, 2 * PC_PER_LAYER + c: 2 * PC_PER_LAYER + c + 1]
                    S.add("dve", lambda e, c=c, g=g, k=k, r=r: e.scalar_tensor_tensor(out=stg[k], in0=xT[:, c, :], scalar=g, in1=r[:], op0=ALU.mult, op1=ALU.mult),
                          reads=[B_x[c], rb, B_prm], writes=[B_stg[k]])
                else:
                    S.add("dve", lambda e, c=c, k=k: e.tensor_copy(out=stg[k], in_=xT[:, c, :]), reads=[B_x[c]], writes=[B_stg[k]])
                S.add("sp", lambda e, c=c, k=k, tok0=tok0: e.dma_start(out=dst[:, c, tok0:tok0 + T], in_=stg[k]),
                      reads=[B_stg[k]], writes=[B_out], dma_sem=sem_out[k])
            alias_sync(B_stg, B_h[0:4])
        final_waits = [(s_, s_.count) for s_ in sem_out]
        for e_ in ("pe", "act", "dve", "pool"):
            assert not S.pending_nosig[e_], e_

        with nc.Block() as block:
            @block.sync
            def _(e):
                S.emit("sp", e)
                for s_, v_ in final_waits:
                    e.wait_ge(s_.h, v_)

            @block.tensor
            def _(e):
                S.emit("pe", e)

            @block.scalar
            def _(e):
                S.emit("act", e)

            @block.vector
            def _(e):
                S.emit("dve", e)

            @block.gpsimd
            def _(e):
                S.emit("pool", e)
    return nc


def make_in_maps(inp, n_cores=NCORES, depth=2):
    x = np.asarray(inp["x"], np.float32)
    prm = pack_params(inp, depth)
    shared = {
        "prm": prm,
        "w_gate1": np.asarray(inp["ffn1_w_gate"], np.float32), "w_gate2": np.asarray(inp["ffn2_w_gate"], np.float32),
        "w_up1": np.asarray(inp["ffn1_w_up"], np.float32), "w_up2": np.asarray(inp["ffn2_w_up"], np.float32),
        "w_down1": np.asarray(inp["ffn1_w_down"], np.float32), "w_down2": np.asarray(inp["ffn2_w_down"], np.float32),
        "w_in": np.asarray(inp["w_in"], np.float32), "w_out": np.asarray(inp["w_out"], np.float32),
        "gate_a_w": np.asarray(inp["lru_gate_a_w"], np.float32), "gate_x_w": np.asarray(inp["lru_gate_x_w"], np.float32),
        "rel_bias": np.ascontiguousarray(np.asarray(inp["rel_bias"], np.float32).reshape(16, 257)),
    }
    maps = []
    for i in range(n_cores):
        xi = x[2 * i:2 * i + 2].reshape(TOK_PER_CORE, D)
        m = dict(shared)
        m["xT"] = np.ascontiguousarray(xi.T)
        maps.append(m)
    return maps


def kernel(**inputs):
    nc = build()
    maps = make_in_maps(inputs)
    res = run_bass_kernel_spmd(nc, maps, core_ids=list(range(NCORES)))
    outs = []
    for i in range(NCORES):
        oT = np.asarray(res.results[i]["outT"])
        outs.append(np.ascontiguousarray(oT.T).reshape(2, SEQ, D))
    return np.concatenate(outs, axis=0).astype(np.float32)
```

```python
import numpy as np
import concourse.bass as bass
import concourse.mybir as mybir
from concourse.bass_utils import run_bass_kernel_spmd

F32 = mybir.dt.float32
BF16 = mybir.dt.bfloat16
AF = mybir.ActivationFunctionType
ALU = mybir.AluOpType

D = 2048
NC_ = 16
DFF = 5632
NJ = 44
HALF = 22
DLRU = 1024
T = 512
SEQ = 2048
NCORES = 8
TOK_PER_CORE = 4096
EPS = 1e-6
SCALE = 128 ** -0.5
PADL = 256


class Sem:
    def __init__(self, handle, name):
        self.h = handle
        self.name = name
        self.count = 0


class Buf:
    __slots__ = ("name", "w", "r")

    def __init__(self, name):
        self.name = name
        self.w = None
        self.r = []


class Sched:
    ENG = ("pe", "act", "dve", "pool", "sp")

    def __init__(self):
        self.ops = {e: [] for e in self.ENG}
        self.waited = {e: {} for e in self.ENG}
        self.esem = {}
        self.pending_nosig = {e: False for e in self.ENG}

    def add(self, eng, fn, reads=(), writes=(), sig=True, dma_sem=None, extra=()):
        waits = {}

        def need(ev):
            if ev is None:
                return
            s, v = ev
            if waits.get(s, 0) < v:
                waits[s] = v

        for b in reads:
            need(b.w)
        for b in writes:
            need(b.w)
            for r in b.r:
                need(r)
        for ev in extra:
            need(ev)
        wl = []
        wd = self.waited[eng]
        own = self.esem.get(eng)
        for s, v in waits.items():
            if eng == "pe" and s is own:
                continue
            if wd.get(s, 0) >= v:
                continue
            wd[s] = v
            wl.append((s, v))
        inc = None
        if dma_sem is not None:
            dma_sem.count += 16
            ev = (dma_sem, dma_sem.count)
            inc = (dma_sem, 16)
        else:
            s = self.esem[eng]
            if sig:
                s.count += 1
                ev = (s, s.count)
                inc = (s, 1)
                self.pending_nosig[eng] = False
            else:
                ev = (s, s.count + 1)
                self.pending_nosig[eng] = True
        self.ops[eng].append((fn, wl, inc))
        for b in writes:
            b.w = ev
            b.r = []
        for b in reads:
            if b in writes:
                continue
            rs = [r for r in b.r if r[0] is not ev[0]]
            rs.append(ev)
            b.r = rs
        return ev

    def emit(self, eng, e):
        for fn, wl, inc in self.ops[eng]:
            for s, v in wl:
                e.wait_ge(s.h, v)
            ins = fn(e)
            if inc is not None:
                ins.then_inc(inc[0].h, inc[1])


def _pcol(v, nch):
    return np.ascontiguousarray(np.asarray(v, np.float32).reshape(nch, 128).T)


PC_PER_LAYER = 128
PC = {}
_o = 0
for _n, _w in (("g1", 16), ("gm", 16), ("g2", 16), ("cw", 32), ("cb", 8), ("ba", 8), ("bx", 8),
               ("lam", 8), ("ga", 8), ("gb", 8)):
    PC[_n] = (_o, _w)
    _o += _w
assert _o == PC_PER_LAYER
NPCOL = 2 * PC_PER_LAYER + 16


def pack_params(inp, depth):
    P = np.zeros((128, NPCOL), np.float32)
    for l in range(depth):
        base = l * PC_PER_LAYER

        def put(name, arr):
            o, w = PC[name]
            P[:, base + o: base + o + w] = arr

        put("g1", _pcol(inp["ffn1_norm"][l], 16))
        put("gm", _pcol(inp["mix_norm"][l], 16))
        put("g2", _pcol(inp["ffn2_norm"][l], 16))
        cw = np.concatenate([_pcol(inp["conv_w"][l][k], 8) for k in range(4)], axis=1)
        put("cw", cw)
        put("cb", _pcol(inp["conv_b"][l], 8))
        put("ba", _pcol(inp["lru_gate_a_b"][l], 8))
        put("bx", _pcol(inp["lru_gate_x_b"][l], 8))
        put("lam", _pcol(inp["lru_lambda"][l], 8))
        put("ga", _pcol(inp["lru_out_norm"][l], 8))
        put("gb", _pcol(inp["att_out_norm"][l], 8))
    P[:, 2 * PC_PER_LAYER:] = _pcol(inp["final_norm"], 16)
    return P


def build(n_tiles=8, n_layers=2, phases=("ffn1", "mix", "ffn2"), final_norm=True, scan_eng="pool", cut=99, pro=("gw", "der", "m")):
    nc = bass.Bass("TRN2", target_bir_lowering=False)
    L = n_layers
    dt_in = lambda name, shape: nc.dram_tensor(name, shape, F32, kind="ExternalInput").ap()
    xT_d = dt_in("xT", [D, TOK_PER_CORE])
    prm_d = dt_in("prm", [128, NPCOL])
    wg_d = [dt_in("w_gate1", [2, D, DFF]), dt_in("w_gate2", [2, D, DFF])]
    wu_d = [dt_in("w_up1", [2, D, DFF]), dt_in("w_up2", [2, D, DFF])]
    wd_d = [dt_in("w_down1", [2, DFF, D]), dt_in("w_down2", [2, DFF, D])]
    win_d = dt_in("w_in", [2, D, 5120])
    wout_d = dt_in("w_out", [2, D, D])
    gaw_d = dt_in("gate_a_w", [2, 16, 64, 64])
    gxw_d = dt_in("gate_x_w", [2, 16, 64, 64])
    rel_d = dt_in("rel_bias", [16, 257])
    out_d = nc.dram_tensor("outT", [D, TOK_PER_CORE], F32, kind="ExternalOutput").ap()

    def scratch(name, shape, dt=BF16):
        return nc.dram_tensor(name, shape, dt, kind="Internal").ap()

    WGU = scratch("s_wgu", [L, 2, NJ, 128, 2, NC_, 128])
    WD = scratch("s_wd", [L, 2, NC_, 2, 128, HALF, 128])
    WIN = scratch("s_win", [L, 20, 128, 2, NC_, 128])
    WOUT = scratch("s_wout", [L, 8, 128, 2, NC_, 128])
    GW = scratch("s_gw", [L, 128, 2, 8, 128])
    RS = scratch("s_r", [16, 128, 768], F32)
    MS = scratch("s_m", [128, 16, 640], F32)
    KD = scratch("s_kd", [L, 128, 8, T])
    VD = scratch("s_vd", [L, 128, 4, 1024])

    S = Sched()
    import contextlib
    es = contextlib.ExitStack()
    with es:
        def sb(name, shape, dt):
            return es.enter_context(nc.sbuf_tensor("sb_" + name, shape, dt))

        def new_sem(name):
            return Sem(es.enter_context(nc.semaphore(name)), name)

        for e in ("pe", "act", "dve", "pool"):
            S.esem[e] = new_sem("e_" + e)
        S.esem["sp"] = None

        xT = sb("xT", [128, NC_, T], F32)
        hT = sb("hT", [128, NC_, T], BF16)
        U1 = sb("U1", [128, 8224], F32)
        gg = U1[:, 0:4096].rearrange("p (c t) -> p c t", t=T)
        xl = U1[:, 4096:8224].rearrange("p (c t) -> p c t", t=T + 4)
        AT = U1[:, 0:5632].bitcast(BF16).rearrange("p (j t) -> p j t", t=T)
        qT = sb("qT", [128, 8, T], BF16)
        kTc = sb("kTc", [128, 8, T], BF16)
        Vc = sb("Vc", [128, 4, 1024], BF16)
        kTp1 = sb("kTp", [128, 8, T], BF16)
        Vp1 = sb("Vp", [128, 4, 1024], BF16)
        kTp = [kTp1 for l in range(L)]
        Vp = [Vp1 for l in range(L)]
        Mh = [sb(f"Mh{i}", [128, 640], F32) for i in range(2)]
        gwt = sb("gwt", [128, 2, 8, 128], BF16)
        NA, NB = 3, 2
        ringA = [sb(f"rA{i}", [128, 2, NC_, 128], BF16) for i in range(NA)]
        ringB = [sb(f"rB{i}", [128, HALF, 128], BF16) for i in range(NB)]
        NTMP = 3
        tmp = [sb(f"tmp{i}", [128, T], F32) for i in range(NTMP)]
        AbufS = [[sb(f"Ab{k}_{i}", [128, PADL + T], F32) for i in range(2)] for k in range(2)]
        BbufS = [sb(f"Bb{k}", [128, PADL + T], F32) for k in range(2)]
        ltmp = [[sb(f"lt{k}_{i}", [128, T], F32) for i in range(3)] for k in range(2)]
        ptb = [sb(f"pt{i}", [128, T], BF16) for i in range(2)]
        xcbS = [sb(f"xcb{k}", [128, T], BF16) for k in range(2)]
        prm = sb("prm", [128, NPCOL], F32)
        der = sb("der", [128, 2, 24], F32)
        hst = sb("hst", [128, 2, 8], F32)
        ctail = sb("ctail", [128, 2, 8, 4], F32)
        ones_b = sb("ones_b", [128, 128], BF16)
        stg = [hT[:, 2 * k:2 * k + 2, :].rearrange("p a t -> p (a t)").bitcast(F32) for k in range(2)]
        psum = [es.enter_context(nc.psum_tensor(f"ps{i}", [128, T], F32)) for i in range(8)]

        B_x = [Buf(f"x{c}") for c in range(NC_)]
        B_h = [Buf(f"h{c}") for c in range(NC_)]
        B_at = [Buf(f"at{j}") for j in range(HALF)]
        B_gg = [Buf(f"gg{c}") for c in range(8)]
        B_xl = [Buf(f"xl{c}") for c in range(8)]
        B_q = [Buf(f"q{c}") for c in range(8)]
        B_kc = [Buf(f"kc{c}") for c in range(8)]
        B_vc = [Buf(f"vc{c}") for c in range(4)]
        _bkp = [Buf(f"kp{c}") for c in range(8)]
        _bvp = [Buf(f"vp{c}") for c in range(4)]
        B_kp = [_bkp for l in range(L)]
        B_vp = [_bvp for l in range(L)]
        B_kd = [Buf(f"kd{l}") for l in range(L)]
        B_vd = [Buf(f"vd{l}") for l in range(L)]
        B_M = [Buf("M0"), Buf("M1")]
        B_gw = Buf("gw")
        B_rA = [Buf(f"rA{i}") for i in range(NA)]
        B_rB = [Buf(f"rB{i}") for i in range(NB)]
        B_tmp = [Buf(f"tmp{i}") for i in range(NTMP)]
        B_AS = [[Buf(f"A{k}_{i}") for i in range(2)] for k in range(2)]
        B_BS = [Buf("B0"), Buf("B1")]
        B_lt = [[Buf(f"lt{k}_{i}") for i in range(3)] for k in range(2)]
        B_pt = [Buf("pt0"), Buf("pt1")]
        B_xcbS = [Buf("xcb0"), Buf("xcb1")]
        B_prm = Buf("prm")
        B_der = Buf("der")
        B_hst = [[Buf(f"hst{l}_{c}") for c in range(8)] for l in range(L)]
        B_ct = [[Buf(f"ct{l}_{c}") for c in range(8)] for l in range(L)]
        B_ones = Buf("ones")
        B_stg = [Buf("stg0"), Buf("stg1")]
        B_ps = [Buf(f"ps{i}") for i in range(8)]
        B_out = Buf("out")
        B_cv = {}

        sem_rA = [new_sem(f"rA{i}") for i in range(NA)]
        sem_rB = [new_sem(f"rB{i}") for i in range(NB)]
        sem_M = [new_sem("M0"), new_sem("M1")]
        sem_gw = new_sem("gw")
        sem_kst, sem_vst, sem_kld, sem_vld = new_sem("kst"), new_sem("vst"), new_sem("kld"), new_sem("vld")
        sem_x = [new_sem(f"xld{i}") for i in range(4)]
        sem_out = [new_sem("ost0"), new_sem("ost1")]
        _mc = [0]

        def misc_sem():
            _mc[0] += 1
            return new_sem(f"misc{_mc[0]}")
        sem_prm = new_sem("prm")

        def col(l, name, i=0):
            o, _ = PC[name]
            k = l * PC_PER_LAYER + o + i
            return prm[:, k:k + 1]

        def alias_sync(srcs, dsts):
            evs = []
            for sbf in srcs:
                if sbf.w is not None:
                    evs.append(sbf.w)
                evs.extend(sbf.r)
            for d_ in dsts:
                d_.r = list(d_.r) + evs

        S.add("sp", lambda e: e.dma_start(out=prm[:], in_=prm_d), writes=[B_prm], dma_sem=sem_prm)
        S.add("dve", lambda e: e.memset(ones_b[:], 1.0), writes=[B_ones])
        S.add("dve", lambda e: e.memset(hst[:], 0.0), writes=[b for l in range(L) for b in B_hst[l]])
        S.add("dve", lambda e: e.memset(ctail[:], 0.0), writes=[b for l in range(L) for b in B_ct[l]])
        for k in range(2):
            for i in range(2):
                S.add("dve", (lambda k, i: lambda e: e.memset(AbufS[k][i][:, 0:PADL], 1.0))(k, i), writes=[B_AS[k][i]])
            S.add("dve", (lambda k: lambda e: e.memset(BbufS[k][:, 0:PADL], 0.0))(k), writes=[B_BS[k]])

        def conv_group(key, dmas):
            sem = new_sem("cv_" + key)
            b = Buf("cv_" + key)
            for fn in dmas:
                S.add("pool", fn, writes=[], dma_sem=sem)
            b.w = (sem, sem.count)
            B_cv[key] = b

        def cv(dst, src):
            return lambda e: e.dma_start(out=dst, in_=src)

        def conv_ffn(l, f):
            gsrc = wg_d[f][l].rearrange("(c p) (j q) -> j p c q", p=128, q=128)
            usrc = wu_d[f][l].rearrange("(c p) (j q) -> j p c q", p=128, q=128)
            dm = []
            for j0 in range(NJ):
                dm.append(cv(WGU[l, f, j0, :, 0], gsrc[j0]))
                dm.append(cv(WGU[l, f, j0, :, 1], usrc[j0]))
            conv_group(f"gu{l}{f}", dm)
            dsrc = wd_d[f][l].rearrange("(h jj p) (c q) -> c h p jj q", h=2, p=128, q=128)
            dm = []
            for c in range(NC_):
                for h in range(2):
                    dm.append(cv(WD[l, f, c, h], dsrc[c, h]))
            conv_group(f"d{l}{f}", dm)

        for l in range(L):
            for ph in phases:
                if ph == "ffn1":
                    conv_ffn(l, 0)
                elif ph == "ffn2":
                    conv_ffn(l, 1)
                else:
                    isrc = win_d[l].rearrange("(c p) (jp two q) -> jp p two c q", p=128, two=2, q=128)
                    dm = [cv(WIN[l, j0, :, tw], isrc[j0, :, tw]) for j0 in range(20) for tw in range(2)]
                    conv_group(f"in{l}", dm)
                    osrc = wout_d[l].rearrange("(c p) (jp two q) -> jp p two c q", p=128, two=2, q=128)
                    dm = [cv(WOUT[l, j0, :, tw], osrc[j0, :, tw]) for j0 in range(8) for tw in range(2)]
                    conv_group(f"out{l}", dm)

        if "mix" in phases:
            S.add("dve", lambda e: e.memset(gwt[:], 0.0), writes=[B_gw])
            B_gws = Buf("gws")
            for l in (range(L) if "gw" in pro else []):
                S.add("sp", (lambda l: lambda e: e.dma_start(out=GW[l], in_=gwt[:]))(l), reads=[B_gw],
                      writes=[B_gws], dma_sem=misc_sem())
            semg = new_sem("cv_gw")
            for l in (range(L) if "gw" in pro else []):
                for g, src in ((0, gaw_d), (1, gxw_d)):
                    sr = src[l].rearrange("(cc e) i j -> e i cc j", e=2)
                    for e_ in range(2):
                        dst = GW[l, e_ * 64:(e_ + 1) * 64, g, :, e_ * 64:(e_ + 1) * 64]
                        S.add("pool", cv(dst, sr[e_]), reads=[], writes=[], dma_sem=semg, extra=[B_gws.w])
            B_gws.w = (semg, semg.count)

            for l in (range(L) if "der" in pro else []):
                lam = prm[:, l * PC_PER_LAYER + PC["lam"][0]: l * PC_PER_LAYER + PC["lam"][0] + 8]
                _tl = tmp + [ltmp[0][0]]
                _bl = B_tmp + [B_lt[0][0]]
                t0, t1, t2, t3 = (_tl[i][:, 0:8] for i in range(4))
                bt = [_bl[i] for i in range(4)]
                dd = der[:, l, 0:8]
                S.add("dve", lambda e, lam=lam, t0=t0: e.tensor_scalar(out=t0, in0=lam, scalar1=-1.0, scalar2=None, op0=ALU.mult),
                      reads=[B_prm], writes=[bt[0]])
                S.add("dve", lambda e, lam=lam, t0=t0: e.tensor_tensor(out=t0, in0=t0, in1=lam, op=ALU.max),
                      reads=[B_prm, bt[0]], writes=[bt[0]])
                S.add("act", lambda e, t0=t0, t1=t1: e.activation(out=t1, in_=t0, func=AF.Exp, scale=-1.0),
                      reads=[bt[0]], writes=[bt[1]])
                S.add("dve", lambda e, t1=t1, t2=t2: e.tensor_scalar(out=t2, in0=t1, scalar1=2.0, scalar2=None, op0=ALU.add),
                      reads=[bt[1]], writes=[bt[2]])
                S.add("dve", lambda e, t2=t2: e.reciprocal(out=t2, in_=t2), reads=[bt[2]], writes=[bt[2]])
                S.add("dve", lambda e, t1=t1, t2=t2: e.tensor_tensor(out=t2, in0=t2, in1=t1, op=ALU.mult),
                      reads=[bt[1], bt[2]], writes=[bt[2]])
                S.add("dve", lambda e, t2=t2, t3=t3: e.tensor_tensor(out=t3, in0=t2, in1=t2, op=ALU.mult),
                      reads=[bt[2]], writes=[bt[3]])
                S.add("dve", lambda e, t3=t3, t1=t1: e.tensor_scalar(out=t1, in0=t3, scalar1=1.0 / 11, scalar2=1.0 / 9, op0=ALU.mult, op1=ALU.add),
                      reads=[bt[3]], writes=[bt[1]])
                for cst in (1.0 / 7, 1.0 / 5, 1.0 / 3, 1.0):
                    S.add("dve", lambda e, t3=t3, t1=t1: e.tensor_tensor(out=t1, in0=t1, in1=t3, op=ALU.mult),
                          reads=[bt[1], bt[3]], writes=[bt[1]])
                    S.add("dve", lambda e, t1=t1, cst=cst: e.tensor_scalar(out=t1, in0=t1, scalar1=cst, scalar2=None, op0=ALU.add),
                          reads=[bt[1]], writes=[bt[1]])
                S.add("dve", lambda e, t1=t1, t2=t2: e.tensor_tensor(out=t1, in0=t1, in1=t2, op=ALU.mult),
                      reads=[bt[1], bt[2]], writes=[bt[1]])
                S.add("dve", lambda e, lam=lam, t0=t0: e.tensor_scalar(out=t0, in0=lam, scalar1=-1.0, scalar2=0.0, op0=ALU.mult, op1=ALU.max),
                      reads=[B_prm], writes=[bt[0]])
                S.add("dve", lambda e, t1=t1: e.tensor_scalar(out=t1, in0=t1, scalar1=-16.0, scalar2=None, op0=ALU.mult),
                      reads=[bt[1]], writes=[bt[1]])
                S.add("dve", lambda e, t0=t0, t1=t1, dd=dd: e.scalar_tensor_tensor(out=dd, in0=t0, scalar=-8.0, in1=t1, op0=ALU.mult, op1=ALU.add),
                      reads=[bt[0], bt[1]], writes=[B_der])
                for nm, o in (("ba", 8), ("bx", 16)):
                    src = prm[:, l * PC_PER_LAYER + PC[nm][0]: l * PC_PER_LAYER + PC[nm][0] + 8]
                    S.add("dve", lambda e, src=src, o=o, l=l: e.tensor_scalar(out=der[:, l, o:o + 8], in0=src, scalar1=-1.0, scalar2=None, op0=ALU.mult),
                          reads=[B_prm, B_der], writes=[B_der])

            tbflat = U1[:, 4096:4096 + 16 * 257]
            ext4 = U1[:, 0:3072].rearrange("p (a j) -> p a j", j=768)
            mrb = hT[:, 0:10, :].rearrange("p a t -> p (a t)").bitcast(F32).rearrange("p (a u) -> p a u", u=640)
            B_tb, B_ext, B_mrb, B_rs, B_ms = Buf("tb"), Buf("ext"), Buf("mrb"), Buf("rs"), Buf("ms")
            S.add("sp", lambda e: e.dma_start(out=tbflat, in_=bass.AP(rel_d.tensor, 0, [[0, 128], [1, 16 * 257]])),
                  writes=[B_tb], dma_sem=misc_sem())
            for g4 in (range(4) if "m" in pro else []):
                for i in range(4):
                    lh = g4 * 4 + i
                    rev = bass.AP(tbflat.tensor, tbflat[:, lh * 257 + 255:lh * 257 + 256].offset, [list(tbflat.ap[0]), [-1, 256]])
                    S.add("dve", lambda e, i=i, rev=rev: e.tensor_copy(out=ext4[:, i, 0:256], in_=rev), reads=[B_tb], writes=[B_ext])
                    t0c = tbflat[:, lh * 257:lh * 257 + 1]
                    S.add("dve", lambda e, i=i, t0c=t0c: e.tensor_scalar(out=ext4[:, i, 256:768], in0=tbflat[:, 0:512], scalar1=0.0, scalar2=t0c, op0=ALU.mult, op1=ALU.add),
                          reads=[B_tb, B_ext], writes=[B_ext])
                S.add("act", lambda e: e.activation(out=ext4, in_=ext4, func=AF.Exp), reads=[B_ext], writes=[B_ext])
                S.add("sp", lambda e, g4=g4: e.dma_start(out=RS[g4 * 4:(g4 + 1) * 4].rearrange("a p j -> p a j"), in_=ext4),
                      reads=[B_ext], writes=[B_rs], dma_sem=misc_sem())
                srcm = bass.AP(RS.tensor, g4 * 4 * 128 * 768 + 127, [[767, 128], [128 * 768, 4], [1, 640]])
                S.add("sp", lambda e, srcm=srcm: e.dma_start(out=mrb, in_=srcm), reads=[B_rs], writes=[B_mrb], dma_sem=misc_sem())
                S.add("dve", lambda e: e.memset(mrb[0:64, :, 576:640], 0.0), reads=[B_mrb], writes=[B_mrb])
                S.add("dve", lambda e: e.memset(mrb[64:128, :, 0:64], 0.0), reads=[B_mrb], writes=[B_mrb])
                S.add("sp", lambda e, g4=g4: e.dma_start(out=MS[:, g4 * 4:(g4 + 1) * 4, :], in_=mrb), reads=[B_mrb], writes=[B_ms], dma_sem=misc_sem())
            B_ms_all = Buf("ms_all")
            B_ms_all.w = B_ms.w
            alias_sync([B_tb, B_ext, B_mrb], B_at + B_gg + B_xl + B_h)

        def tile_info(ti):
            return ti // 4, ti % 4

        planA, planB = [], []
        for ti in range(n_tiles):
            for l in range(L):
                for ph in phases:
                    if ph in ("ffn1", "ffn2"):
                        f = 0 if ph == "ffn1" else 1
                        for h in range(2):
                            for jj in range(HALF):
                                planA.append(("gu", l, f, h * HALF + jj))
                            for c in range(NC_):
                                planB.append((l, f, c, h))
                    else:
                        for jp in range(20):
                            planA.append(("in", l, jp))
                        for jp in range(8):
                            planA.append(("out", l, jp))
        stA = {"next_load": 0, "next_use": 0}
        stB = {"next_load": 0, "next_use": 0}

        def loadA():
            k = stA["next_load"]
            if k >= len(planA):
                return
            stA["next_load"] += 1
            u = planA[k]
            slot = k % NA
            if u[0] == "gu":
                src = WGU[u[1], u[2], u[3]]
                cvb = B_cv[f"gu{u[1]}{u[2]}"]
            elif u[0] == "in":
                src = WIN[u[1], u[2]]
                cvb = B_cv[f"in{u[1]}"]
            else:
                src = WOUT[u[1], u[2]]
                cvb = B_cv[f"out{u[1]}"]
            S.add("sp", lambda e, slot=slot, src=src: e.dma_start(out=ringA[slot][:], in_=src),
                  writes=[B_rA[slot]], dma_sem=sem_rA[slot], extra=[cvb.w])

        def loadB():
            k = stB["next_load"]
            if k >= len(planB):
                return
            stB["next_load"] += 1
            l, f, c, h = planB[k]
            slot = k % NB
            src = WD[l, f, c, h]
            S.add("sp", lambda e, slot=slot, src=src: e.dma_start(out=ringB[slot][:], in_=src),
                  writes=[B_rB[slot]], dma_sem=sem_rB[slot], extra=[B_cv[f"d{l}{f}"].w])

        def useA(expect):
            k = stA["next_use"]
            assert planA[k] == expect, (planA[k], expect)
            stA["next_use"] += 1
            return k % NA

        def useB(expect):
            k = stB["next_use"]
            assert planB[k] == expect, (planB[k], expect)
            stB["next_use"] += 1
            return k % NB

        for _ in range(NA):
            loadA()
        for _ in range(NB):
            loadB()

        tcnt = {"t": 0, "ps": 0}

        def tmp_next():
            i = tcnt["t"] % NTMP
            tcnt["t"] += 1
            return tmp[i], B_tmp[i]

        def rms_stats(srcs, nd, ps_i):
            n = len(srcs)
            for i, (ap, b) in enumerate(srcs):
                k = i % 2
                S.add("dve", lambda e, ap=ap, k=k: e.tensor_tensor(out=ptb[k][:], in0=ap, in1=ap, op=ALU.mult),
                      reads=[b], writes=[B_pt[k]])
                S.add("pe", lambda e, k=k, i=i, n=n: e.matmul(psum[ps_i][:], lhsT=ones_b[:], rhs=ptb[k][:], start=(i == 0), stop=(i == n - 1)),
                      reads=[B_pt[k], B_ones], writes=[B_ps[ps_i]], sig=True)
            r, rb = tmp_next()
            S.add("act", lambda e, r=r: e.activation(out=r[:], in_=psum[ps_i][:], func=AF.Ln, scale=1.0 / nd, bias=EPS),
                  reads=[B_ps[ps_i]], writes=[rb])
            S.add("act", lambda e, r=r: e.activation(out=r[:], in_=r[:], func=AF.Exp, scale=-0.5), reads=[rb], writes=[rb])
            return r, rb

        def norm_to_h(l, gname, gbase=None):
            r, rb = rms_stats([(xT[:, c, :], B_x[c]) for c in range(NC_)], float(D), 6)
            for c in range(NC_):
                g = col(l, gname, c) if gbase is None else prm[:, gbase + c: gbase + c + 1]
                S.add("dve", lambda e, c=c, g=g, r=r: e.scalar_tensor_tensor(out=hT[:, c, :], in0=xT[:, c, :], scalar=g, in1=r[:], op0=ALU.mult, op1=ALU.mult),
                      reads=[B_x[c], rb, B_prm], writes=[B_h[c]])

        def ffn(l, f):
            gname = "g1" if f == 0 else "g2"
            alias_sync(B_gg + B_xl, B_at)
            norm_to_h(l, gname)
            for h in range(2):
                for jj in range(HALF):
                    j = h * HALF + jj
                    slot = useA(("gu", l, f, j))
                    pg, pu = j % 2, 2 + j % 2
                    for which, pi in ((0, pg), (1, pu)):
                        for c in range(NC_):
                            S.add("pe", lambda e, slot=slot, which=which, c=c, pi=pi: e.matmul(
                                psum[pi][:], lhsT=ringA[slot][:, which, c, :], rhs=hT[:, c, :], start=(c == 0), stop=(c == NC_ - 1)),
                                reads=[B_rA[slot], B_h[c]], writes=[B_ps[pi]], sig=(c == NC_ - 1))
                    loadA()
                    sg, sgb = tmp_next()
                    S.add("act", lambda e, sg=sg, pg=pg: e.activation(out=sg[:], in_=psum[pg][:], func=AF.Silu),
                          reads=[B_ps[pg]], writes=[sgb])
                    S.add("dve", lambda e, sg=sg, pu=pu, jj=jj: e.tensor_tensor(out=AT[:, jj, :], in0=sg[:], in1=psum[pu][:], op=ALU.mult),
                          reads=[sgb, B_ps[pu]], writes=[B_at[jj]])
                for c in range(NC_):
                    slot = useB((l, f, c, h))
                    py = 4 + c % 2
                    for jj in range(HALF):
                        S.add("pe", lambda e, slot=slot, jj=jj, py=py: e.matmul(
                            psum[py][:], lhsT=ringB[slot][:, jj, :], rhs=AT[:, jj, :], start=(jj == 0), stop=(jj == HALF - 1)),
                            reads=[B_rB[slot], B_at[jj]], writes=[B_ps[py]], sig=(jj == HALF - 1))
                    loadB()
                    S.add("dve", lambda e, c=c, py=py: e.scalar_tensor_tensor(out=xT[:, c, :], in0=psum[py][:], scalar=0.5, in1=xT[:, c, :], op0=ALU.mult, op1=ALU.add),
                          reads=[B_ps[py], B_x[c]], writes=[B_x[c]])

        def mixer(l, first_in_seq):
            alias_sync(B_at, B_gg + B_xl)
            norm_to_h(l, "gm")
            S.add("sp", lambda e: e.dma_start(out=gwt[:], in_=GW[l]), writes=[B_gw], dma_sem=sem_gw, extra=[B_gws.w])
            if not first_in_seq:
                S.add("sp", lambda e: e.dma_start(out=kTp1[:], in_=KD[l]), reads=[B_kd[l]], writes=_bkp, dma_sem=sem_kld)
                S.add("sp", lambda e: e.dma_start(out=Vp1[:], in_=VD[l]), reads=[B_vd[l]], writes=_bvp, dma_sem=sem_vld)

            def W_feat(jp):
                slot = useA(("in", l, jp))
                for which in range(2):
                    jcol = jp * 2 + which
                    pi = jcol % 2
                    for c in range(NC_):
                        S.add("pe", lambda e, slot=slot, which=which, c=c, pi=pi: e.matmul(
                            psum[pi][:], lhsT=ringA[slot][:, which, c, :], rhs=hT[:, c, :], start=(c == 0), stop=(c == NC_ - 1)),
                            reads=[B_rA[slot], B_h[c]], writes=[B_ps[pi]], sig=(c == NC_ - 1))
                    kind, idx = jcol // 8, jcol % 8
                    if kind == 0:
                        S.add("act", lambda e, idx=idx, pi=pi: e.activation(out=xl[:, idx, 3:3 + T], in_=psum[pi][:], func=AF.Copy),
                              reads=[B_ps[pi]], writes=[B_xl[idx]])
                        S.add("pool", lambda e, idx=idx: e.tensor_copy(out=xl[:, idx, 0:3], in_=ctail[:, l, idx, 0:3]),
                              reads=[B_ct[l][idx]], writes=[B_xl[idx]])
                    elif kind == 1:
                        g = gg[:, idx, :]
                        t1, t1b = tmp_next()
                        S.add("act", lambda e, g=g, pi=pi: e.activation(out=g, in_=psum[pi][:], func=AF.Copy),
                              reads=[B_ps[pi]], writes=[B_gg[idx]])
                        S.add("dve", lambda e, g=g, t1=t1: e.tensor_tensor(out=t1[:], in0=g, in1=g, op=ALU.mult),
                              reads=[B_gg[idx]], writes=[t1b])
                        S.add("dve", lambda e, t1=t1: e.tensor_scalar(out=t1[:], in0=t1[:], scalar1=0.044715, scalar2=1.0, op0=ALU.mult, op1=ALU.add),
                              reads=[t1b], writes=[t1b])
                        S.add("dve", lambda e, g=g, t1=t1: e.tensor_tensor(out=t1[:], in0=t1[:], in1=g, op=ALU.mult),
                              reads=[t1b, B_gg[idx]], writes=[t1b])
                        S.add("act", lambda e, t1=t1: e.activation(out=t1[:], in_=t1[:], func=AF.Exp, scale=-1.5957691216057308),
                              reads=[t1b], writes=[t1b])
                        S.add("dve", lambda e, t1=t1: e.tensor_scalar(out=t1[:], in0=t1[:], scalar1=1.0, scalar2=None, op0=ALU.add),
                              reads=[t1b], writes=[t1b])
                        S.add("dve", lambda e, t1=t1: e.reciprocal(out=t1[:], in_=t1[:]), reads=[t1b], writes=[t1b])
                        S.add("dve", lambda e, g=g, t1=t1: e.tensor_tensor(out=g, in0=g, in1=t1[:], op=ALU.mult),
                              reads=[t1b, B_gg[idx]], writes=[B_gg[idx]])
                    elif kind == 2:
                        S.add("act", lambda e, idx=idx, pi=pi: e.activation(out=qT[:, idx, :], in_=psum[pi][:], func=AF.Copy),
                              reads=[B_ps[pi]], writes=[B_q[idx]])
                    else:
                        S.add("dve", lambda e, idx=idx, pi=pi: e.tensor_copy(out=kTc[:, idx, :], in_=psum[pi][:]),
                              reads=[B_ps[pi]], writes=[B_kc[idx]])
                loadA()

            def V_unit(jq, jpp):
                jp = 16 + jq * 2 + jpp
                slot = useA(("in", l, jp))
                for which in range(2):
                    jc = jpp * 2 + which
                    for tb in range(4):
                        pi = 2 + tb
                        for c in range(NC_):
                            S.add("pe", lambda e, slot=slot, which=which, c=c, pi=pi, jc=jc, tb=tb: e.matmul(
                                psum[pi][:, jc * 128:(jc + 1) * 128], lhsT=hT[:, c, tb * 128:(tb + 1) * 128], rhs=ringA[slot][:, which, c, :],
                                start=(c == 0), stop=(c == NC_ - 1)),
                                reads=[B_rA[slot], B_h[c]], writes=[B_ps[pi]], sig=(c == NC_ - 1))
                loadA()
                if jpp == 1:
                    for tb in range(4):
                        pi = 2 + tb
                        if tb % 2 == 0:
                            S.add("act", lambda e, tb=tb, pi=pi, jq=jq: e.activation(out=Vc[:, tb, jq * 512:(jq + 1) * 512], in_=psum[pi][:], func=AF.Copy),
                                  reads=[B_ps[pi]], writes=[B_vc[tb]])
                        else:
                            S.add("dve", lambda e, tb=tb, pi=pi, jq=jq: e.tensor_copy(out=Vc[:, tb, jq * 512:(jq + 1) * 512], in_=psum[pi][:]),
                                  reads=[B_ps[pi]], writes=[B_vc[tb]])

            SE = scan_eng

            def lru_stages(cc):
                k = cc % 2
                A0, A1 = AbufS[k]
                bA = B_AS[k]
                Bm = BbufS[k]
                bB = B_BS[k]
                xcbk, bxcb = xcbS[k], B_xcbS[k]
                (xc, ra, ri), (xcbuf, rab, rib) = ltmp[k], B_lt[k]
                cw = [col(l, "cw", kk * 8 + cc) for kk in range(4)]
                pg = (6, 7)

                def s1():
                    S.add("dve", lambda e: e.tensor_scalar(out=xc[:], in0=xl[:, cc, 0:T], scalar1=cw[0], scalar2=col(l, "cb", cc), op0=ALU.mult, op1=ALU.add),
                          reads=[B_xl[cc], B_prm], writes=[xcbuf])
                    for kk in (1, 2, 3):
                        S.add("dve", lambda e, kk=kk: e.scalar_tensor_tensor(out=xc[:], in0=xl[:, cc, kk:kk + T], scalar=cw[kk], in1=xc[:], op0=ALU.mult, op1=ALU.add),
                              reads=[B_xl[cc], xcbuf, B_prm], writes=[xcbuf])
                    S.add("pool", lambda e: e.tensor_copy(out=ctail[:, l, cc, 0:3], in_=xl[:, cc, T:T + 3]),
                          reads=[B_xl[cc]], writes=[B_ct[l][cc]])
                    S.add("act", lambda e: e.activation(out=xcbk[:], in_=xc[:], func=AF.Copy), reads=[xcbuf], writes=[bxcb])

                def s2():
                    for g in range(2):
                        S.add("pe", lambda e, g=g: e.matmul(psum[pg[g]][:], lhsT=gwt[:, g, cc, :], rhs=xcbk[:], start=True, stop=True),
                              reads=[B_gw, bxcb], writes=[B_ps[pg[g]]])
                    for g, (rt, rtb) in enumerate(((ra, rab), (ri, rib))):
                        nb = der[:, l, 8 + 8 * g + cc: 8 + 8 * g + cc + 1]
                        S.add("act", lambda e, g=g, rt=rt, nb=nb: e.activation(out=rt[:], in_=psum[pg[g]][:], func=AF.Exp, scale=-1.0, bias=nb),
                              reads=[B_ps[pg[g]], B_der], writes=[rtb])

                def s3():
                    for rt, rtb in ((ra, rab), (ri, rib)):
                        S.add("dve", lambda e, rt=rt: e.tensor_scalar(out=rt[:], in0=rt[:], scalar1=1.0, scalar2=None, op0=ALU.add), reads=[rtb], writes=[rtb])
                        S.add("dve", lambda e, rt=rt: e.reciprocal(out=rt[:], in_=rt[:]), reads=[rtb], writes=[rtb])
                    S.add("act", lambda e: e.activation(out=A0[:, PADL:], in_=ra[:], func=AF.Exp, scale=der[:, l, cc:cc + 1]),
                          reads=[rab, B_der], writes=[bA[0]])

                def s4():
                    S.add("dve", lambda e: e.tensor_tensor(out=ra[:], in0=A0[:, PADL:], in1=A0[:, PADL:], op=ALU.mult),
                          reads=[bA[0]], writes=[rab])
                    S.add("dve", lambda e: e.tensor_scalar(out=ra[:], in0=ra[:], scalar1=-1.0, scalar2=1.0, op0=ALU.mult, op1=ALU.add), reads=[rab], writes=[rab])
                    S.add("dve", lambda e: e.tensor_scalar(out=ra[:], in0=ra[:], scalar1=1e-30, scalar2=None, op0=ALU.max), reads=[rab], writes=[rab])
                    S.add("act", lambda e: e.activation(out=ra[:], in_=ra[:], func=AF.Ln), reads=[rab], writes=[rab])
                    S.add("act", lambda e: e.activation(out=ra[:], in_=ra[:], func=AF.Exp, scale=0.5), reads=[rab], writes=[rab])
                    S.add("dve", lambda e: e.tensor_tensor(out=ri[:], in0=ri[:], in1=xc[:], op=ALU.mult), reads=[rib, xcbuf], writes=[rib])

                def s5():
                    S.add("dve", lambda e: e.tensor_tensor(out=Bm[:, PADL:], in0=ra[:], in1=ri[:], op=ALU.mult),
                          reads=[rab, rib], writes=[bB])
                    S.add("dve", lambda e: e.scalar_tensor_tensor(out=Bm[:, PADL:PADL + 1], in0=A0[:, PADL:PADL + 1], scalar=hst[:, l, cc:cc + 1],
                                                                  in1=Bm[:, PADL:PADL + 1], op0=ALU.mult, op1=ALU.add),
                          reads=[bA[0], bB, B_hst[l][cc]], writes=[bB])
                    Ab = (A0, A1)
                    cur = 0
                    dsteps = [1, 2, 4, 8, 16, 32, 64, 128, 256]
                    for si, dd_ in enumerate(dsteps):
                        S.add(SE, lambda e, cur=cur, dd_=dd_: e.tensor_tensor(out=ra[:], in0=Ab[cur][:, PADL:], in1=Bm[:, PADL - dd_:PADL + T - dd_], op=ALU.mult),
                              reads=[bA[cur], bB], writes=[rab])
                        S.add(SE, lambda e: e.tensor_tensor(out=Bm[:, PADL:], in0=Bm[:, PADL:], in1=ra[:], op=ALU.add),
                              reads=[rab, bB], writes=[bB])
                        if si < len(dsteps) - 1:
                            S.add(SE, lambda e, cur=cur, dd_=dd_: e.tensor_tensor(out=Ab[1 - cur][:, PADL:], in0=Ab[cur][:, PADL:], in1=Ab[cur][:, PADL - dd_:PADL + T - dd_], op=ALU.mult),
                                  reads=[bA[cur]], writes=[bA[1 - cur]])
                            cur = 1 - cur
                    assert cur == 0
                    S.add("pool", lambda e: e.tensor_copy(out=hst[:, l, cc:cc + 1], in_=Bm[:, PADL + T - 1:PADL + T]),
                          reads=[bB], writes=[B_hst[l][cc]])

                def fin():
                    S.add("dve", lambda e: e.tensor_tensor(out=gg[:, cc, :], in0=gg[:, cc, :], in1=Bm[:, PADL:], op=ALU.mult),
                          reads=[bB, B_gg[cc]], writes=[B_gg[cc]])

                return [s1, s2, s3, s4, s5], fin

            bg = []
            fins = []
            for cc in range(8):
                st_, fn_ = lru_stages(cc)
                if cc >= 2:
                    bg.append(fins[cc - 2])
                bg.extend(st_)
                fins.append(fn_)
            bg.append(fins[6])
            bg.append(fins[7])
            bgi = [0]

            def bg_step(n=1):
                for _ in range(n):
                    if bgi[0] < len(bg):
                        bg[bgi[0]]()
                        bgi[0] += 1

            blocks = [4, 5, 6, 7] + ([] if first_in_seq else [3, 2, 1, 0])
            seq = [(h, b) for h in range(8) for b in blocks]

            def kblock(h, b):
                if b >= 4:
                    return kTc[:, h, (b - 4) * 128:(b - 3) * 128], B_kc[h]
                return kTp1[:, h, b * 128:(b + 1) * 128], _bkp[h]

            def vblock(h, b):
                if b >= 4:
                    return Vc[:, b - 4, h * 128:(h + 1) * 128], B_vc[b - 4]
                return Vp1[:, b, h * 128:(h + 1) * 128], _bvp[b]

            def qrange(b):
                return max(0, 128 * b - 512), min(512, 128 * b + 128)

            def emit_S(k):
                h, b = seq[k]
                qs, qe = qrange(b)
                kap, kb = kblock(h, b)
                pi = k % 2
                S.add("pe", lambda e, kap=kap, h=h, qs=qs, qe=qe, pi=pi: e.matmul(psum[pi][:, 0:qe - qs], lhsT=kap, rhs=qT[:, h, qs:qe], start=True, stop=True),
                      reads=[kb, B_q[h]], writes=[B_ps[pi]])

            def load_M(h):
                mi = h % 2
                S.add("sp", lambda e, h=h, mi=mi: e.dma_start(out=Mh[mi][:], in_=MS[:, l * 8 + h, :]), writes=[B_M[mi]], dma_sem=sem_M[mi],
                      extra=[B_ms_all.w])

            def emit_PV(k):
                h, b = seq[k]
                qs, qe = qrange(b)
                n = qe - qs
                pi = k % 2
                us = qs - 128 * b + 512
                mi = h % 2
                et, etb = tmp_next()
                S.add("act", lambda e, et=et, pi=pi, n=n: e.activation(out=et[:, 0:n], in_=psum[pi][:, 0:n], func=AF.Exp, scale=SCALE),
                      reads=[B_ps[pi]], writes=[etb])
                S.add("dve", lambda e, et=et, pi=pi, n=n, us=us, mi=mi: e.tensor_tensor(out=ptb[pi][:, 0:n], in0=et[:, 0:n], in1=Mh[mi][:, us:us + n], op=ALU.mult),
                      reads=[etb, B_M[mi]], writes=[B_pt[pi]])
                vap, vb = vblock(h, b)
                po, pd = 2 + h % 2, 4 + h % 2
                first = (b == blocks[0])
                last = (b == blocks[-1])
                S.add("pe", lambda e, vap=vap, pi=pi, n=n, qs=qs, qe=qe, po=po, first=first, last=last: e.matmul(
                    psum[po][:, qs:qe], lhsT=vap, rhs=ptb[pi][:, 0:n], start=first, stop=last),
                    reads=[vb, B_pt[pi]], writes=[B_ps[po]], sig=False)
                S.add("pe", lambda e, pi=pi, n=n, qs=qs, qe=qe, pd=pd, first=first, last=last: e.matmul(
                    psum[pd][:, qs:qe], lhsT=ones_b[:], rhs=ptb[pi][:, 0:n], start=first, stop=last),
                    reads=[B_ones, B_pt[pi]], writes=[B_ps[pd]], sig=True)
                if last:
                    rd, rdb = tmp_next()
                    S.add("dve", lambda e, rd=rd, pd=pd: e.reciprocal(out=rd[:], in_=psum[pd][:]), reads=[B_ps[pd]], writes=[rdb])
                    S.add("dve", lambda e, rd=rd, po=po, h=h: e.tensor_tensor(out=xl[:, h, 0:T], in0=rd[:], in1=psum[po][:], op=ALU.mult),
                          reads=[rdb, B_ps[po]], writes=[B_xl[h]])
                    if h + 2 < 8:
                        load_M(h + 2)

            for jp in range(8):
                W_feat(jp)
            bg_step(2)
            for jp in range(8, 16):
                W_feat(jp)
                bg_step(1)
            if cut < 2:
                return
            for jq in range(2):
                for jpp in range(2):
                    V_unit(jq, jpp)
                    bg_step(2)
            if cut < 4:
                return
            load_M(0)
            load_M(1)
            emit_S(0)
            npairs = len(seq)
            nbg_left = len(bg) - bgi[0]
            every = max(1, npairs // max(1, nbg_left + 2))
            for k in range(npairs):
                if k + 1 < npairs:
                    emit_S(k + 1)
                emit_PV(k)
                if k % every == every - 1:
                    bg_step(1)
            while bgi[0] < len(bg):
                bg_step(1)
            if cut < 5:
                return
            S.add("sp", lambda e: e.dma_start(out=KD[l], in_=kTc[:]), reads=B_kc, writes=[B_kd[l]], dma_sem=sem_kst)
            S.add("sp", lambda e: e.dma_start(out=VD[l], in_=Vc[:]), reads=B_vc, writes=[B_vd[l]], dma_sem=sem_vst)
            if cut < 6:
                return
            rA, rAb = rms_stats([(gg[:, c, :], B_gg[c]) for c in range(8)], float(DLRU), 6)
            rB, rBb = rms_stats([(xl[:, c, 0:T], B_xl[c]) for c in range(8)], float(DLRU), 7)
            for c in range(8):
                S.add("dve", lambda e, c=c: e.scalar_tensor_tensor(out=hT[:, c, :], in0=gg[:, c, :], scalar=col(l, "ga", c), in1=rA[:], op0=ALU.mult, op1=ALU.mult),
                      reads=[B_gg[c], rAb, B_prm], writes=[B_h[c]])
            for c in range(8):
                S.add("dve", lambda e, c=c: e.scalar_tensor_tensor(out=hT[:, 8 + c, :], in0=xl[:, c, 0:T], scalar=col(l, "gb", c), in1=rB[:], op0=ALU.mult, op1=ALU.mult),
                      reads=[B_xl[c], rBb, B_prm], writes=[B_h[8 + c]])
            for jp in range(8):
                slot = useA(("out", l, jp))
                for which in range(2):
                    dc = jp * 2 + which
                    pi = dc % 2
                    for c in range(NC_):
                        S.add("pe", lambda e, slot=slot, which=which, c=c, pi=pi: e.matmul(
                            psum[pi][:], lhsT=ringA[slot][:, which, c, :], rhs=hT[:, c, :], start=(c == 0), stop=(c == NC_ - 1)),
                            reads=[B_rA[slot], B_h[c]], writes=[B_ps[pi]], sig=(c == NC_ - 1))
                    S.add("dve", lambda e, dc=dc, pi=pi: e.tensor_tensor(out=xT[:, dc, :], in0=xT[:, dc, :], in1=psum[pi][:], op=ALU.add),
                          reads=[B_ps[pi], B_x[dc]], writes=[B_x[dc]])
                loadA()

        for ti in range(n_tiles):
            sq, pos = tile_info(ti)
            tok0 = ti * T
            src = xT_d.rearrange("(c p) t -> p c t", p=128)
            for c4 in range(4):
                S.add("sp", lambda e, c4=c4, tok0=tok0: e.dma_start(out=xT[:, c4 * 4:(c4 + 1) * 4, :], in_=src[:, c4 * 4:(c4 + 1) * 4, tok0:tok0 + T]),
                      writes=B_x[c4 * 4:(c4 + 1) * 4], dma_sem=sem_x[c4])
            if pos == 0 and "mix" in phases and ti > 0:
                S.add("dve", lambda e: e.memset(hst[:], 0.0), writes=[b for l in range(L) for b in B_hst[l]])
                S.add("dve", lambda e: e.memset(ctail[:], 0.0), writes=[b for l in range(L) for b in B_ct[l]])
            for l in range(L):
                for ph in phases:
                    if ph == "ffn1":
                        ffn(l, 0)
                    elif ph == "ffn2":
                        ffn(l, 1)
                    else:
                        mixer(l, pos == 0)
            if final_norm:
                r, rb = rms_stats([(xT[:, c, :], B_x[c]) for c in range(NC_)], float(D), 6)
            alias_sync(B_h[0:4], B_stg)
            dst = out_d.rearrange("(c p) t -> p c t", p=128)
            for c in range(NC_):
                k = c % 2
                if final_norm:
                    g = prm[:, 2 * PC_PER_LAYER + c: 2 * PC_PER_LAYER + c + 1]
                    S.add("dve", lambda e, c=c, g=g, k=k, r=r: e.scalar_tensor_tensor(out=stg[k], in0=xT[:, c, :], scalar=g, in1=r[:], op0=ALU.mult, op1=ALU.mult),
                          reads=[B_x[c], rb, B_prm], writes=[B_stg[k]])
                else:
                    S.add("dve", lambda e, c=c, k=k: e.tensor_copy(out=stg[k], in_=xT[:, c, :]), reads=[B_x[c]], writes=[B_stg[k]])
                S.add("sp", lambda e, c=c, k=k, tok0=tok0: e.dma_start(out=dst[:, c, tok0:tok0 + T], in_=stg[k]),
                      reads=[B_stg[k]], writes=[B_out], dma_sem=sem_out[k])
            alias_sync(B_stg, B_h[0:4])
        final_waits = [(s_, s_.count) for s_ in sem_out]
        for e_ in ("pe", "act", "dve", "pool"):
            assert not S.pending_nosig[e_], e_

        with nc.Block() as block:
            @block.sync
            def _(e):
                S.emit("sp", e)
                for s_, v_ in final_waits:
                    e.wait_ge(s_.h, v_)

            @block.tensor
            def _(e):
                S.emit("pe", e)

            @block.scalar
            def _(e):
                S.emit("act", e)

            @block.vector
            def _(e):
                S.emit("dve", e)

            @block.gpsimd
            def _(e):
                S.emit("pool", e)
    return nc


def make_in_maps(inp, n_cores=NCORES, depth=2):
    x = np.asarray(inp["x"], np.float32)
    prm = pack_params(inp, depth)
    shared = {
        "prm": prm,
        "w_gate1": np.asarray(inp["ffn1_w_gate"], np.float32), "w_gate2": np.asarray(inp["ffn2_w_gate"], np.float32),
        "w_up1": np.asarray(inp["ffn1_w_up"], np.float32), "w_up2": np.asarray(inp["ffn2_w_up"], np.float32),
        "w_down1": np.asarray(inp["ffn1_w_down"], np.float32), "w_down2": np.asarray(inp["ffn2_w_down"], np.float32),
        "w_in": np.asarray(inp["w_in"], np.float32), "w_out": np.asarray(inp["w_out"], np.float32),
        "gate_a_w": np.asarray(inp["lru_gate_a_w"], np.float32), "gate_x_w": np.asarray(inp["lru_gate_x_w"], np.float32),
        "rel_bias": np.ascontiguousarray(np.asarray(inp["rel_bias"], np.float32).reshape(16, 257)),
    }
    maps = []
    for i in range(n_cores):
        xi = x[2 * i:2 * i + 2].reshape(TOK_PER_CORE, D)
        m = dict(shared)
        m["xT"] = np.ascontiguousarray(xi.T)
        maps.append(m)
    return maps


def kernel(**inputs):
    nc = build()
    maps = make_in_maps(inputs)
    res = run_bass_kernel_spmd(nc, maps, core_ids=list(range(NCORES)))
    outs = []
    for i in range(NCORES):
        oT = np.asarray(res.results[i]["outT"])
        outs.append(np.ascontiguousarray(oT.T).reshape(2, SEQ, D))
    return np.concatenate(outs, axis=0).astype(np.float32)
```

```python
import numpy as np
import concourse.bass as bass
import concourse.mybir as mybir
from concourse.bass_utils import run_bass_kernel_spmd

F32 = mybir.dt.float32
BF16 = mybir.dt.bfloat16
AF = mybir.ActivationFunctionType
ALU = mybir.AluOpType

D = 2048
NC_ = 16
DFF = 5632
NJ = 44
HALF = 22
DLRU = 1024
T = 512
SEQ = 2048
NCORES = 8
TOK_PER_CORE = 4096
EPS = 1e-6
SCALE = 128 ** -0.5
PADL = 256


class Sem:
    def __init__(self, handle, name):
        self.h = handle
        self.name = name
        self.count = 0


class Buf:
    __slots__ = ("name", "w", "r")

    def __init__(self, name):
        self.name = name
        self.w = None
        self.r = []


class Sched:
    ENG = ("pe", "act", "dve", "pool", "sp")

    def __init__(self):
        self.ops = {e: [] for e in self.ENG}
        self.waited = {e: {} for e in self.ENG}
        self.esem = {}
        self.pending_nosig = {e: False for e in self.ENG}

    def add(self, eng, fn, reads=(), writes=(), sig=True, dma_sem=None, extra=()):
        waits = {}

        def need(ev):
            if ev is None:
                return
            s, v = ev
            if waits.get(s, 0) < v:
                waits[s] = v

        for b in reads:
            need(b.w)
        for b in writes:
            need(b.w)
            for r in b.r:
                need(r)
        for ev in extra:
            need(ev)
        wl = []
        wd = self.waited[eng]
        own = self.esem.get(eng)
        for s, v in waits.items():
            if eng == "pe" and s is own:
                continue
            if wd.get(s, 0) >= v:
                continue
            wd[s] = v
            wl.append((s, v))
        inc = None
        if dma_sem is not None:
            dma_sem.count += 16
            ev = (dma_sem, dma_sem.count)
            inc = (dma_sem, 16)
        else:
            s = self.esem[eng]
            if sig:
                s.count += 1
                ev = (s, s.count)
                inc = (s, 1)
                self.pending_nosig[eng] = False
            else:
                ev = (s, s.count + 1)
                self.pending_nosig[eng] = True
        self.ops[eng].append((fn, wl, inc))
        for b in writes:
            b.w = ev
            b.r = []
        for b in reads:
            if b in writes:
                continue
            rs = [r for r in b.r if r[0] is not ev[0]]
            rs.append(ev)
            b.r = rs
        return ev

    def emit(self, eng, e):
        for fn, wl, inc in self.ops[eng]:
            for s, v in wl:
                e.wait_ge(s.h, v)
            ins = fn(e)
            if inc is not None:
                ins.then_inc(inc[0].h, inc[1])


def _pcol(v, nch):
    return np.ascontiguousarray(np.asarray(v, np.float32).reshape(nch, 128).T)


PC_PER_LAYER = 128
PC = {}
_o = 0
for _n, _w in (("g1", 16), ("gm", 16), ("g2", 16), ("cw", 32), ("cb", 8), ("ba", 8), ("bx", 8),
               ("lam", 8), ("ga", 8), ("gb", 8)):
    PC[_n] = (_o, _w)
    _o += _w
assert _o == PC_PER_LAYER
NPCOL = 2 * PC_PER_LAYER + 16


def pack_params(inp, depth):
    P = np.zeros((128, NPCOL), np.float32)
    for l in range(depth):
        base = l * PC_PER_LAYER

        def put(name, arr):
            o, w = PC[name]
            P[:, base + o: base + o + w] = arr

        put("g1", _pcol(inp["ffn1_norm"][l], 16))
        put("gm", _pcol(inp["mix_norm"][l], 16))
        put("g2", _pcol(inp["ffn2_norm"][l], 16))
        cw = np.concatenate([_pcol(inp["conv_w"][l][k], 8) for k in range(4)], axis=1)
        put("cw", cw)
        put("cb", _pcol(inp["conv_b"][l], 8))
        put("ba", _pcol(inp["lru_gate_a_b"][l], 8))
        put("bx", _pcol(inp["lru_gate_x_b"][l], 8))
        put("lam", _pcol(inp["lru_lambda"][l], 8))
        put("ga", _pcol(inp["lru_out_norm"][l], 8))
        put("gb", _pcol(inp["att_out_norm"][l], 8))
    P[:, 2 * PC_PER_LAYER:] = _pcol(inp["final_norm"], 16)
    return P


def build(n_tiles=8, n_layers=2, phases=("ffn1", "mix", "ffn2"), final_norm=True, scan_eng="pool", cut=99, pro=("gw", "der", "m")):
    nc = bass.Bass("TRN2", target_bir_lowering=False)
    L = n_layers
    dt_in = lambda name, shape: nc.dram_tensor(name, shape, F32, kind="ExternalInput").ap()
    xT_d = dt_in("xT", [D, TOK_PER_CORE])
    prm_d = dt_in("prm", [128, NPCOL])
    wg_d = [dt_in("w_gate1", [2, D, DFF]), dt_in("w_gate2", [2, D, DFF])]
    wu_d = [dt_in("w_up1", [2, D, DFF]), dt_in("w_up2", [2, D, DFF])]
    wd_d = [dt_in("w_down1", [2, DFF, D]), dt_in("w_down2", [2, DFF, D])]
    win_d = dt_in("w_in", [2, D, 5120])
    wout_d = dt_in("w_out", [2, D, D])
    gaw_d = dt_in("gate_a_w", [2, 16, 64, 64])
    gxw_d = dt_in("gate_x_w", [2, 16, 64, 64])
    rel_d = dt_in("rel_bias", [16, 257])
    out_d = nc.dram_tensor("outT", [D, TOK_PER_CORE], F32, kind="ExternalOutput").ap()

    def scratch(name, shape, dt=BF16):
        return nc.dram_tensor(name, shape, dt, kind="Internal").ap()

    WGU = scratch("s_wgu", [L, 2, NJ, 128, 2, NC_, 128])
    WD = scratch("s_wd", [L, 2, NC_, 2, 128, HALF, 128])
    WIN = scratch("s_win", [L, 20, 128, 2, NC_, 128])
    WOUT = scratch("s_wout", [L, 8, 128, 2, NC_, 128])
    GW = scratch("s_gw", [L, 128, 2, 8, 128])
    RS = scratch("s_r", [16, 128, 768], F32)
    MS = scratch("s_m", [128, 16, 640], F32)
    KD = scratch("s_kd", [L, 128, 8, T])
    VD = scratch("s_vd", [L, 128, 4, 1024])

    S = Sched()
    import contextlib
    es = contextlib.ExitStack()
    with es:
        def sb(name, shape, dt):
            return es.enter_context(nc.sbuf_tensor("sb_" + name, shape, dt))

        def new_sem(name):
            return Sem(es.enter_context(nc.semaphore(name)), name)

        for e in ("pe", "act", "dve", "pool"):
            S.esem[e] = new_sem("e_" + e)
        S.esem["sp"] = None

        xT = sb("xT", [128, NC_, T], F32)
        hT = sb("hT", [128, NC_, T], BF16)
        U1 = sb("U1", [128, 8224], F32)
        gg = U1[:, 0:4096].rearrange("p (c t) -> p c t", t=T)
        xl = U1[:, 4096:8224].rearrange("p (c t) -> p c t", t=T + 4)
        AT = U1[:, 0:5632].bitcast(BF16).rearrange("p (j t) -> p j t", t=T)
        qT = sb("qT", [128, 8, T], BF16)
        kTc = sb("kTc", [128, 8, T], BF16)
        Vc = sb("Vc", [128, 4, 1024], BF16)
        kTp1 = sb("kTp", [128, 8, T], BF16)
        Vp1 = sb("Vp", [128, 4, 1024], BF16)
        kTp = [kTp1 for l in range(L)]
        Vp = [Vp1 for l in range(L)]
        Mh = [sb(f"Mh{i}", [128, 640], F32) for i in range(2)]
        gwt = sb("gwt", [128, 2, 8, 128], BF16)
        NA, NB = 3, 2
        ringA = [sb(f"rA{i}", [128, 2, NC_, 128], BF16) for i in range(NA)]
        ringB = [sb(f"rB{i}", [128, HALF, 128], BF16) for i in range(NB)]
        NTMP = 3
        tmp = [sb(f"tmp{i}", [128, T], F32) for i in range(NTMP)]
        AbufS = [[sb(f"Ab{k}_{i}", [128, 8, 96], F32) for i in range(2)] for k in range(2)]
        BbufS = [sb(f"Bb{k}", [128, 8, 96], F32) for k in range(2)]
        ltmp = [[sb(f"lt{k}_{i}", [128, T], F32) for i in range(3)] for k in range(2)]
        ptb = [sb(f"pt{i}", [128, T], BF16) for i in range(2)]
        xcbS = [sb(f"xcb{k}", [128, T], BF16) for k in range(2)]
        prm = sb("prm", [128, NPCOL], F32)
        der = sb("der", [128, 2, 24], F32)
        hst = sb("hst", [128, 2, 8], F32)
        ctail = sb("ctail", [128, 2, 8, 4], F32)
        ones_b = sb("ones_b", [128, 128], BF16)
        stg = [hT[:, 2 * k:2 * k + 2, :].rearrange("p a t -> p (a t)").bitcast(F32) for k in range(2)]
        psum = [es.enter_context(nc.psum_tensor(f"ps{i}", [128, T], F32)) for i in range(8)]

        B_x = [Buf(f"x{c}") for c in range(NC_)]
        B_h = [Buf(f"h{c}") for c in range(NC_)]
        B_at = [Buf(f"at{j}") for j in range(HALF)]
        B_gg = [Buf(f"gg{c}") for c in range(8)]
        B_xl = [Buf(f"xl{c}") for c in range(8)]
        B_q = [Buf(f"q{c}") for c in range(8)]
        B_kc = [Buf(f"kc{c}") for c in range(8)]
        B_vc = [Buf(f"vc{c}") for c in range(4)]
        _bkp = [Buf(f"kp{c}") for c in range(8)]
        _bvp = [Buf(f"vp{c}") for c in range(4)]
        B_kp = [_bkp for l in range(L)]
        B_vp = [_bvp for l in range(L)]
        B_kd = [Buf(f"kd{l}") for l in range(L)]
        B_vd = [Buf(f"vd{l}") for l in range(L)]
        B_M = [Buf("M0"), Buf("M1")]
        B_gw = Buf("gw")
        B_rA = [Buf(f"rA{i}") for i in range(NA)]
        B_rB = [Buf(f"rB{i}") for i in range(NB)]
        B_tmp = [Buf(f"tmp{i}") for i in range(NTMP)]
        B_AS = [[Buf(f"A{k}_{i}") for i in range(2)] for k in range(2)]
        B_BS = [Buf("B0"), Buf("B1")]
        B_lt = [[Buf(f"lt{k}_{i}") for i in range(3)] for k in range(2)]
        B_pt = [Buf("pt0"), Buf("pt1")]
        B_xcbS = [Buf("xcb0"), Buf("xcb1")]
        B_prm = Buf("prm")
        B_der = Buf("der")
        B_hst = [[Buf(f"hst{l}_{c}") for c in range(8)] for l in range(L)]
        B_ct = [[Buf(f"ct{l}_{c}") for c in range(8)] for l in range(L)]
        B_ones = Buf("ones")
        B_stg = [Buf("stg0"), Buf("stg1")]
        B_ps = [Buf(f"ps{i}") for i in range(8)]
        B_out = Buf("out")
        B_cv = {}

        sem_rA = [new_sem(f"rA{i}") for i in range(NA)]
        sem_rB = [new_sem(f"rB{i}") for i in range(NB)]
        sem_M = [new_sem("M0"), new_sem("M1")]
        sem_gw = new_sem("gw")
        sem_kst, sem_vst, sem_kld, sem_vld = new_sem("kst"), new_sem("vst"), new_sem("kld"), new_sem("vld")
        sem_x = [new_sem(f"xld{i}") for i in range(4)]
        sem_out = [new_sem("ost0"), new_sem("ost1")]
        _mc = [0]

        def misc_sem():
            _mc[0] += 1
            return new_sem(f"misc{_mc[0]}")
        sem_prm = new_sem("prm")

        def col(l, name, i=0):
            o, _ = PC[name]
            k = l * PC_PER_LAYER + o + i
            return prm[:, k:k + 1]

        def alias_sync(srcs, dsts):
            evs = []
            for sbf in srcs:
                if sbf.w is not None:
                    evs.append(sbf.w)
                evs.extend(sbf.r)
            for d_ in dsts:
                d_.r = list(d_.r) + evs

        S.add("sp", lambda e: e.dma_start(out=prm[:], in_=prm_d), writes=[B_prm], dma_sem=sem_prm)
        S.add("dve", lambda e: e.memset(ones_b[:], 1.0), writes=[B_ones])
        S.add("dve", lambda e: e.memset(hst[:], 0.0), writes=[b for l in range(L) for b in B_hst[l]])
        S.add("dve", lambda e: e.memset(ctail[:], 0.0), writes=[b for l in range(L) for b in B_ct[l]])
        for k in range(2):
            for i in range(2):
                S.add("dve", (lambda k, i: lambda e: e.memset(AbufS[k][i][:, :, 0:32], 1.0))(k, i), writes=[B_AS[k][i]])
            S.add("dve", (lambda k: lambda e: e.memset(BbufS[k][:, :, 0:32], 0.0))(k), writes=[B_BS[k]])

        def conv_group(key, dmas):
            sem = new_sem("cv_" + key)
            b = Buf("cv_" + key)
            for fn in dmas:
                S.add("pool", fn, writes=[], dma_sem=sem)
            b.w = (sem, sem.count)
            B_cv[key] = b

        def cv(dst, src):
            return lambda e: e.dma_start(out=dst, in_=src)

        def conv_ffn(l, f):
            gsrc = wg_d[f][l].rearrange("(c p) (j q) -> j p c q", p=128, q=128)
            usrc = wu_d[f][l].rearrange("(c p) (j q) -> j p c q", p=128, q=128)
            dsrc = wd_d[f][l].rearrange("(h jj p) (c q) -> c h p jj q", h=2, p=128, q=128)
            for h in range(2):
                dm = []
                for j0 in range(h * HALF, (h + 1) * HALF):
                    dm.append(cv(WGU[l, f, j0, :, 0], gsrc[j0]))
                    dm.append(cv(WGU[l, f, j0, :, 1], usrc[j0]))
                conv_group(f"gu{l}{f}{h}", dm)
                dm = []
                for c in range(NC_):
                    dm.append(cv(WD[l, f, c, h], dsrc[c, h]))
                conv_group(f"d{l}{f}{h}", dm)

        for l in range(L):
            for ph in phases:
                if ph == "ffn1":
                    conv_ffn(l, 0)
                elif ph == "ffn2":
                    conv_ffn(l, 1)
                else:
                    isrc = win_d[l].rearrange("(c p) (jp two q) -> jp p two c q", p=128, two=2, q=128)
                    dm = [cv(WIN[l, j0, :, tw], isrc[j0, :, tw]) for j0 in range(20) for tw in range(2)]
                    conv_group(f"in{l}", dm)
                    osrc = wout_d[l].rearrange("(c p) (jp two q) -> jp p two c q", p=128, two=2, q=128)
                    dm = [cv(WOUT[l, j0, :, tw], osrc[j0, :, tw]) for j0 in range(8) for tw in range(2)]
                    conv_group(f"out{l}", dm)

        if "mix" in phases:
            S.add("dve", lambda e: e.memset(gwt[:], 0.0), writes=[B_gw])
            B_gws = Buf("gws")
            for l in (range(L) if "gw" in pro else []):
                S.add("sp", (lambda l: lambda e: e.dma_start(out=GW[l], in_=gwt[:]))(l), reads=[B_gw],
                      writes=[B_gws], dma_sem=misc_sem())
            semg = new_sem("cv_gw")
            for l in (range(L) if "gw" in pro else []):
                for g, src in ((0, gaw_d), (1, gxw_d)):
                    sr = src[l].rearrange("(cc e) i j -> e i cc j", e=2)
                    for e_ in range(2):
                        dst = GW[l, e_ * 64:(e_ + 1) * 64, g, :, e_ * 64:(e_ + 1) * 64]
                        S.add("pool", cv(dst, sr[e_]), reads=[], writes=[], dma_sem=semg, extra=[B_gws.w])
            B_gws.w = (semg, semg.count)

            for l in (range(L) if "der" in pro else []):
                lam = prm[:, l * PC_PER_LAYER + PC["lam"][0]: l * PC_PER_LAYER + PC["lam"][0] + 8]
                _tl = tmp + [ltmp[0][0]]
                _bl = B_tmp + [B_lt[0][0]]
                t0, t1, t2, t3 = (_tl[i][:, 0:8] for i in range(4))
                bt = [_bl[i] for i in range(4)]
                dd = der[:, l, 0:8]
                S.add("dve", lambda e, lam=lam, t0=t0: e.tensor_scalar(out=t0, in0=lam, scalar1=-1.0, scalar2=None, op0=ALU.mult),
                      reads=[B_prm], writes=[bt[0]])
                S.add("dve", lambda e, lam=lam, t0=t0: e.tensor_tensor(out=t0, in0=t0, in1=lam, op=ALU.max),
                      reads=[B_prm, bt[0]], writes=[bt[0]])
                S.add("act", lambda e, t0=t0, t1=t1: e.activation(out=t1, in_=t0, func=AF.Exp, scale=-1.0),
                      reads=[bt[0]], writes=[bt[1]])
                S.add("dve", lambda e, t1=t1, t2=t2: e.tensor_scalar(out=t2, in0=t1, scalar1=2.0, scalar2=None, op0=ALU.add),
                      reads=[bt[1]], writes=[bt[2]])
                S.add("dve", lambda e, t2=t2: e.reciprocal(out=t2, in_=t2), reads=[bt[2]], writes=[bt[2]])
                S.add("dve", lambda e, t1=t1, t2=t2: e.tensor_tensor(out=t2, in0=t2, in1=t1, op=ALU.mult),
                      reads=[bt[1], bt[2]], writes=[bt[2]])
                S.add("dve", lambda e, t2=t2, t3=t3: e.tensor_tensor(out=t3, in0=t2, in1=t2, op=ALU.mult),
                      reads=[bt[2]], writes=[bt[3]])
                S.add("dve", lambda e, t3=t3, t1=t1: e.tensor_scalar(out=t1, in0=t3, scalar1=1.0 / 11, scalar2=1.0 / 9, op0=ALU.mult, op1=ALU.add),
                      reads=[bt[3]], writes=[bt[1]])
                for cst in (1.0 / 7, 1.0 / 5, 1.0 / 3, 1.0):
                    S.add("dve", lambda e, t3=t3, t1=t1: e.tensor_tensor(out=t1, in0=t1, in1=t3, op=ALU.mult),
                          reads=[bt[1], bt[3]], writes=[bt[1]])
                    S.add("dve", lambda e, t1=t1, cst=cst: e.tensor_scalar(out=t1, in0=t1, scalar1=cst, scalar2=None, op0=ALU.add),
                          reads=[bt[1]], writes=[bt[1]])
                S.add("dve", lambda e, t1=t1, t2=t2: e.tensor_tensor(out=t1, in0=t1, in1=t2, op=ALU.mult),
                      reads=[bt[1], bt[2]], writes=[bt[1]])
                S.add("dve", lambda e, lam=lam, t0=t0: e.tensor_scalar(out=t0, in0=lam, scalar1=-1.0, scalar2=0.0, op0=ALU.mult, op1=ALU.max),
                      reads=[B_prm], writes=[bt[0]])
                S.add("dve", lambda e, t1=t1: e.tensor_scalar(out=t1, in0=t1, scalar1=-16.0, scalar2=None, op0=ALU.mult),
                      reads=[bt[1]], writes=[bt[1]])
                S.add("dve", lambda e, t0=t0, t1=t1, dd=dd: e.scalar_tensor_tensor(out=dd, in0=t0, scalar=-8.0, in1=t1, op0=ALU.mult, op1=ALU.add),
                      reads=[bt[0], bt[1]], writes=[B_der])
                for nm, o in (("ba", 8), ("bx", 16)):
                    src = prm[:, l * PC_PER_LAYER + PC[nm][0]: l * PC_PER_LAYER + PC[nm][0] + 8]
                    S.add("dve", lambda e, src=src, o=o, l=l: e.tensor_scalar(out=der[:, l, o:o + 8], in0=src, scalar1=-1.0, scalar2=None, op0=ALU.mult),
                          reads=[B_prm, B_der], writes=[B_der])

            tbflat = U1[:, 4096:4096 + 16 * 257]
            ext4 = U1[:, 0:3072].rearrange("p (a j) -> p a j", j=768)
            mrb = hT[:, 0:10, :].rearrange("p a t -> p (a t)").bitcast(F32).rearrange("p (a u) -> p a u", u=640)
            B_tb, B_ext, B_mrb, B_rs, B_ms = Buf("tb"), Buf("ext"), Buf("mrb"), Buf("rs"), Buf("ms")
            S.add("sp", lambda e: e.dma_start(out=tbflat, in_=bass.AP(rel_d.tensor, 0, [[0, 128], [1, 16 * 257]])),
                  writes=[B_tb], dma_sem=misc_sem())
            for g4 in (range(4) if "m" in pro else []):
                for i in range(4):
                    lh = g4 * 4 + i
                    rev = bass.AP(tbflat.tensor, tbflat[:, lh * 257 + 255:lh * 257 + 256].offset, [list(tbflat.ap[0]), [-1, 256]])
                    S.add("dve", lambda e, i=i, rev=rev: e.tensor_copy(out=ext4[:, i, 0:256], in_=rev), reads=[B_tb], writes=[B_ext])
                    t0c = tbflat[:, lh * 257:lh * 257 + 1]
                    S.add("dve", lambda e, i=i, t0c=t0c: e.tensor_scalar(out=ext4[:, i, 256:768], in0=tbflat[:, 0:512], scalar1=0.0, scalar2=t0c, op0=ALU.mult, op1=ALU.add),
                          reads=[B_tb, B_ext], writes=[B_ext])
                S.add("act", lambda e: e.activation(out=ext4, in_=ext4, func=AF.Exp), reads=[B_ext], writes=[B_ext])
                S.add("sp", lambda e, g4=g4: e.dma_start(out=RS[g4 * 4:(g4 + 1) * 4].rearrange("a p j -> p a j"), in_=ext4),
                      reads=[B_ext], writes=[B_rs], dma_sem=misc_sem())
                srcm = bass.AP(RS.tensor, g4 * 4 * 128 * 768 + 127, [[767, 128], [128 * 768, 4], [1, 640]])
                S.add("sp", lambda e, srcm=srcm: e.dma_start(out=mrb, in_=srcm), reads=[B_rs], writes=[B_mrb], dma_sem=misc_sem())
                S.add("dve", lambda e: e.memset(mrb[0:64, :, 576:640], 0.0), reads=[B_mrb], writes=[B_mrb])
                S.add("dve", lambda e: e.memset(mrb[64:128, :, 0:64], 0.0), reads=[B_mrb], writes=[B_mrb])
                S.add("sp", lambda e, g4=g4: e.dma_start(out=MS[:, g4 * 4:(g4 + 1) * 4, :], in_=mrb), reads=[B_mrb], writes=[B_ms], dma_sem=misc_sem())
            B_ms_all = Buf("ms_all")
            B_ms_all.w = B_ms.w
            alias_sync([B_tb, B_ext, B_mrb], B_at + B_gg + B_xl + B_h)

        def tile_info(ti):
            return ti // 4, ti % 4

        planA, planB = [], []
        for ti in range(n_tiles):
            for l in range(L):
                for ph in phases:
                    if ph in ("ffn1", "ffn2"):
                        f = 0 if ph == "ffn1" else 1
                        for h in range(2):
                            for jj in range(HALF):
                                planA.append(("gu", l, f, h * HALF + jj))
                            for c in range(NC_):
                                planB.append((l, f, c, h))
                    else:
                        for jp in range(20):
                            planA.append(("in", l, jp))
                        for jp in range(8):
                            planA.append(("out", l, jp))
        stA = {"next_load": 0, "next_use": 0}
        stB = {"next_load": 0, "next_use": 0}

        def loadA():
            k = stA["next_load"]
            if k >= len(planA):
                return
            stA["next_load"] += 1
            u = planA[k]
            slot = k % NA
            if u[0] == "gu":
                src = WGU[u[1], u[2], u[3]]
                cvb = B_cv[f"gu{u[1]}{u[2]}{u[3] // HALF}"]
            elif u[0] == "in":
                src = WIN[u[1], u[2]]
                cvb = B_cv[f"in{u[1]}"]
            else:
                src = WOUT[u[1], u[2]]
                cvb = B_cv[f"out{u[1]}"]
            S.add("sp", lambda e, slot=slot, src=src: e.dma_start(out=ringA[slot][:], in_=src),
                  writes=[B_rA[slot]], dma_sem=sem_rA[slot], extra=[cvb.w])

        def loadB():
            k = stB["next_load"]
            if k >= len(planB):
                return
            stB["next_load"] += 1
            l, f, c, h = planB[k]
            slot = k % NB
            src = WD[l, f, c, h]
            S.add("sp", lambda e, slot=slot, src=src: e.dma_start(out=ringB[slot][:], in_=src),
                  writes=[B_rB[slot]], dma_sem=sem_rB[slot], extra=[B_cv[f"d{l}{f}{h}"].w])

        def useA(expect):
            k = stA["next_use"]
            assert planA[k] == expect, (planA[k], expect)
            stA["next_use"] += 1
            return k % NA

        def useB(expect):
            k = stB["next_use"]
            assert planB[k] == expect, (planB[k], expect)
            stB["next_use"] += 1
            return k % NB

        for _ in range(NA):
            loadA()
        for _ in range(NB):
            loadB()

        tcnt = {"t": 0, "ps": 0}

        def tmp_next():
            i = tcnt["t"] % NTMP
            tcnt["t"] += 1
            return tmp[i], B_tmp[i]

        def rms_stats(srcs, nd, ps_i):
            n = len(srcs)
            for i, (ap, b) in enumerate(srcs):
                k = i % 2
                S.add("dve", lambda e, ap=ap, k=k: e.tensor_tensor(out=ptb[k][:], in0=ap, in1=ap, op=ALU.mult),
                      reads=[b], writes=[B_pt[k]])
                S.add("pe", lambda e, k=k, i=i, n=n: e.matmul(psum[ps_i][:], lhsT=ones_b[:], rhs=ptb[k][:], start=(i == 0), stop=(i == n - 1)),
                      reads=[B_pt[k], B_ones], writes=[B_ps[ps_i]], sig=True)
            r, rb = tmp_next()
            S.add("act", lambda e, r=r: e.activation(out=r[:], in_=psum[ps_i][:], func=AF.Ln, scale=1.0 / nd, bias=EPS),
                  reads=[B_ps[ps_i]], writes=[rb])
            S.add("act", lambda e, r=r: e.activation(out=r[:], in_=r[:], func=AF.Exp, scale=-0.5), reads=[rb], writes=[rb])
            return r, rb

        def norm_to_h(l, gname, gbase=None):
            r, rb = rms_stats([(xT[:, c, :], B_x[c]) for c in range(NC_)], float(D), 6)
            for c in range(NC_):
                g = col(l, gname, c) if gbase is None else prm[:, gbase + c: gbase + c + 1]
                S.add("dve", lambda e, c=c, g=g, r=r: e.scalar_tensor_tensor(out=hT[:, c, :], in0=xT[:, c, :], scalar=g, in1=r[:], op0=ALU.mult, op1=ALU.mult),
                      reads=[B_x[c], rb, B_prm], writes=[B_h[c]])

        def ffn(l, f):
            gname = "g1" if f == 0 else "g2"
            alias_sync(B_gg + B_xl, B_at)
            norm_to_h(l, gname)
            for h in range(2):
                for jj in range(HALF):
                    j = h * HALF + jj
                    slot = useA(("gu", l, f, j))
                    pg, pu = j % 2, 2 + j % 2
                    for which, pi in ((0, pg), (1, pu)):
                        for c in range(NC_):
                            S.add("pe", lambda e, slot=slot, which=which, c=c, pi=pi: e.matmul(
                                psum[pi][:], lhsT=ringA[slot][:, which, c, :], rhs=hT[:, c, :], start=(c == 0), stop=(c == NC_ - 1)),
                                reads=[B_rA[slot], B_h[c]], writes=[B_ps[pi]], sig=(c == NC_ - 1))
                    loadA()
                    sg, sgb = tmp_next()
                    S.add("act", lambda e, sg=sg, pg=pg: e.activation(out=sg[:], in_=psum[pg][:], func=AF.Silu),
                          reads=[B_ps[pg]], writes=[sgb])
                    S.add("dve", lambda e, sg=sg, pu=pu, jj=jj: e.tensor_tensor(out=AT[:, jj, :], in0=sg[:], in1=psum[pu][:], op=ALU.mult),
                          reads=[sgb, B_ps[pu]], writes=[B_at[jj]])
                for c in range(NC_):
                    slot = useB((l, f, c, h))
                    py = 4 + c % 2
                    for jj in range(HALF):
                        S.add("pe", lambda e, slot=slot, jj=jj, py=py: e.matmul(
                            psum[py][:], lhsT=ringB[slot][:, jj, :], rhs=AT[:, jj, :], start=(jj == 0), stop=(jj == HALF - 1)),
                            reads=[B_rB[slot], B_at[jj]], writes=[B_ps[py]], sig=(jj == HALF - 1))
                    loadB()
                    S.add("dve", lambda e, c=c, py=py: e.scalar_tensor_tensor(out=xT[:, c, :], in0=psum[py][:], scalar=0.5, in1=xT[:, c, :], op0=ALU.mult, op1=ALU.add),
                          reads=[B_ps[py], B_x[c]], writes=[B_x[c]])

        def mixer(l, first_in_seq):
            alias_sync(B_at, B_gg + B_xl)
            norm_to_h(l, "gm")
            S.add("sp", lambda e: e.dma_start(out=gwt[:], in_=GW[l]), writes=[B_gw], dma_sem=sem_gw, extra=[B_gws.w])
            if not first_in_seq:
                S.add("sp", lambda e: e.dma_start(out=kTp1[:], in_=KD[l]), reads=[B_kd[l]], writes=_bkp, dma_sem=sem_kld)
                S.add("sp", lambda e: e.dma_start(out=Vp1[:], in_=VD[l]), reads=[B_vd[l]], writes=_bvp, dma_sem=sem_vld)

            def W_feat(jp):
                slot = useA(("in", l, jp))
                for which in range(2):
                    jcol = jp * 2 + which
                    pi = jcol % 2
                    for c in range(NC_):
                        S.add("pe", lambda e, slot=slot, which=which, c=c, pi=pi: e.matmul(
                            psum[pi][:], lhsT=ringA[slot][:, which, c, :], rhs=hT[:, c, :], start=(c == 0), stop=(c == NC_ - 1)),
                            reads=[B_rA[slot], B_h[c]], writes=[B_ps[pi]], sig=(c == NC_ - 1))
                    kind, idx = jcol // 8, jcol % 8
                    if kind == 0:
                        S.add("act", lambda e, idx=idx, pi=pi: e.activation(out=xl[:, idx, 3:3 + T], in_=psum[pi][:], func=AF.Copy),
                              reads=[B_ps[pi]], writes=[B_xl[idx]])
                        S.add("pool", lambda e, idx=idx: e.tensor_copy(out=xl[:, idx, 0:3], in_=ctail[:, l, idx, 0:3]),
                              reads=[B_ct[l][idx]], writes=[B_xl[idx]])
                    elif kind == 1:
                        g = gg[:, idx, :]
                        t1, t1b = tmp_next()
                        S.add("act", lambda e, g=g, pi=pi: e.activation(out=g, in_=psum[pi][:], func=AF.Copy),
                              reads=[B_ps[pi]], writes=[B_gg[idx]])
                        S.add("act", lambda e, t1=t1, pi=pi: e.activation(out=t1[:], in_=psum[pi][:], func=AF.Square),
                              reads=[B_ps[pi]], writes=[t1b])
                        S.add("dve", lambda e, t1=t1: e.tensor_scalar(out=t1[:], in0=t1[:], scalar1=0.044715, scalar2=1.0, op0=ALU.mult, op1=ALU.add),
                              reads=[t1b], writes=[t1b])
                        S.add("dve", lambda e, g=g, t1=t1: e.tensor_tensor(out=t1[:], in0=t1[:], in1=g, op=ALU.mult),
                              reads=[t1b, B_gg[idx]], writes=[t1b])
                        S.add("act", lambda e, t1=t1: e.activation(out=t1[:], in_=t1[:], func=AF.Sigmoid, scale=1.5957691216057308),
                              reads=[t1b], writes=[t1b])
                        S.add("dve", lambda e, g=g, t1=t1: e.tensor_tensor(out=g, in0=g, in1=t1[:], op=ALU.mult),
                              reads=[t1b, B_gg[idx]], writes=[B_gg[idx]])
                    elif kind == 2:
                        S.add("act", lambda e, idx=idx, pi=pi: e.activation(out=qT[:, idx, :], in_=psum[pi][:], func=AF.Copy),
                              reads=[B_ps[pi]], writes=[B_q[idx]])
                    else:
                        S.add("dve", lambda e, idx=idx, pi=pi: e.tensor_copy(out=kTc[:, idx, :], in_=psum[pi][:]),
                              reads=[B_ps[pi]], writes=[B_kc[idx]])
                loadA()

            def V_unit(jq, jpp):
                jp = 16 + jq * 2 + jpp
                slot = useA(("in", l, jp))
                for which in range(2):
                    jc = jpp * 2 + which
                    for tb in range(4):
                        pi = 2 + tb
                        for c in range(NC_):
                            S.add("pe", lambda e, slot=slot, which=which, c=c, pi=pi, jc=jc, tb=tb: e.matmul(
                                psum[pi][:, jc * 128:(jc + 1) * 128], lhsT=hT[:, c, tb * 128:(tb + 1) * 128], rhs=ringA[slot][:, which, c, :],
                                start=(c == 0), stop=(c == NC_ - 1)),
                                reads=[B_rA[slot], B_h[c]], writes=[B_ps[pi]], sig=(c == NC_ - 1))
                loadA()
                if jpp == 1:
                    for tb in range(4):
                        pi = 2 + tb
                        if tb % 2 == 0:
                            S.add("act", lambda e, tb=tb, pi=pi, jq=jq: e.activation(out=Vc[:, tb, jq * 512:(jq + 1) * 512], in_=psum[pi][:], func=AF.Copy),
                                  reads=[B_ps[pi]], writes=[B_vc[tb]])
                        else:
                            S.add("dve", lambda e, tb=tb, pi=pi, jq=jq: e.tensor_copy(out=Vc[:, tb, jq * 512:(jq + 1) * 512], in_=psum[pi][:]),
                                  reads=[B_ps[pi]], writes=[B_vc[tb]])

            SE = scan_eng

            def lru_stages(cc):
                k = cc % 2
                Ab = AbufS[k]
                bA = B_AS[k]
                Bt = BbufS[k]
                bB = B_BS[k]
                Am = [Ab[0][:, :, 32:96], Ab[1][:, :, 32:96]]
                Bm = Bt[:, :, 32:96]
                xcbk, bxcb = xcbS[k], B_xcbS[k]
                (xc, ra, ri), (xcbuf, rab, rib) = ltmp[k], B_lt[k]
                v3 = lambda t_: t_[:].rearrange("p (b t) -> p b t", t=64)
                xc3, ra3, ri3 = v3(xc), v3(ra), v3(ri)
                cw = [col(l, "cw", kk * 8 + cc) for kk in range(4)]
                pg = (6, 7)

                def s1():
                    S.add("dve", lambda e: e.tensor_scalar(out=xc[:], in0=xl[:, cc, 0:T], scalar1=cw[0], scalar2=col(l, "cb", cc), op0=ALU.mult, op1=ALU.add),
                          reads=[B_xl[cc], B_prm], writes=[xcbuf])
                    for kk in (1, 2, 3):
                        S.add("dve", lambda e, kk=kk: e.scalar_tensor_tensor(out=xc[:], in0=xl[:, cc, kk:kk + T], scalar=cw[kk], in1=xc[:], op0=ALU.mult, op1=ALU.add),
                              reads=[B_xl[cc], xcbuf, B_prm], writes=[xcbuf])
                    S.add("pool", lambda e: e.tensor_copy(out=ctail[:, l, cc, 0:3], in_=xl[:, cc, T:T + 3]),
                          reads=[B_xl[cc]], writes=[B_ct[l][cc]])
                    S.add("act", lambda e: e.activation(out=xcbk[:], in_=xc[:], func=AF.Copy), reads=[xcbuf], writes=[bxcb])

                def s2():
                    for g in range(2):
                        S.add("pe", lambda e, g=g: e.matmul(psum[pg[g]][:], lhsT=gwt[:, g, cc, :], rhs=xcbk[:], start=True, stop=True),
                              reads=[B_gw, bxcb], writes=[B_ps[pg[g]]])
                    for g, (rt, rtb, bn) in enumerate(((ra, rab, "ba"), (ri, rib, "bx"))):
                        S.add("act", lambda e, g=g, rt=rt, bn=bn: e.activation(out=rt[:], in_=psum[pg[g]][:], func=AF.Sigmoid, bias=col(l, bn, cc)),
                              reads=[B_ps[pg[g]], B_prm], writes=[rtb])

                def s3():
                    S.add("act", lambda e: e.activation(out=Am[0], in_=ra3, func=AF.Exp, scale=der[:, l, cc:cc + 1]),
                          reads=[rab, B_der], writes=[bA[0]])
                    S.add("act", lambda e: e.activation(out=ra3, in_=Am[0], func=AF.Square), reads=[bA[0]], writes=[rab])
                    S.add("act", lambda e: e.activation(out=ra[:], in_=ra[:], func=AF.Ln, scale=-1.0, bias=1.0), reads=[rab], writes=[rab])
                    S.add("act", lambda e: e.activation(out=ra[:], in_=ra[:], func=AF.Exp, scale=0.5), reads=[rab], writes=[rab])
                    S.add("dve", lambda e: e.tensor_tensor(out=ri[:], in0=ri[:], in1=xc[:], op=ALU.mult), reads=[rib, xcbuf], writes=[rib])

                def s4():
                    S.add("dve", lambda e: e.tensor_tensor(out=Bm, in0=ra3, in1=ri3, op=ALU.mult), reads=[rab, rib], writes=[bB])
                    cur = 0
                    for dd_ in (1, 2, 4, 8, 16, 32):
                        S.add(SE, lambda e, cur=cur, dd_=dd_: e.tensor_tensor(out=ra3, in0=Am[cur], in1=Bt[:, :, 32 - dd_:96 - dd_], op=ALU.mult),
                              reads=[bA[cur], bB], writes=[rab])
                        S.add(SE, lambda e: e.tensor_tensor(out=Bm, in0=Bm, in1=ra3, op=ALU.add), reads=[rab, bB], writes=[bB])
                        S.add(SE, lambda e, cur=cur, dd_=dd_: e.tensor_tensor(out=Am[1 - cur], in0=Am[cur], in1=Ab[cur][:, :, 32 - dd_:96 - dd_], op=ALU.mult),
                              reads=[bA[cur]], writes=[bA[1 - cur]])
                        cur = 1 - cur
                    assert cur == 0

                def fin():
                    for b_ in range(8):
                        carry = hst[:, l, cc:cc + 1] if b_ == 0 else Bt[:, b_ - 1, 95:96]
                        S.add("dve", lambda e, b_=b_, carry=carry: e.scalar_tensor_tensor(out=Bt[:, b_, 32:96], in0=Ab[0][:, b_, 32:96], scalar=carry,
                                                                                          in1=Bt[:, b_, 32:96], op0=ALU.mult, op1=ALU.add),
                              reads=[bA[0], bB, B_hst[l][cc]], writes=[bB])
                    S.add("pool", lambda e: e.tensor_copy(out=hst[:, l, cc:cc + 1], in_=Bt[:, 7, 95:96]), reads=[bB], writes=[B_hst[l][cc]])
                    g3 = gg[:, cc, :].rearrange("p (b t) -> p b t", t=64)
                    S.add("dve", lambda e: e.tensor_tensor(out=g3, in0=g3, in1=Bm, op=ALU.mult), reads=[bB, B_gg[cc]], writes=[B_gg[cc]])

                return [s1, s2, s3, s4], fin

            bg = []
            fins = []
            for cc in range(8):
                st_, fn_ = lru_stages(cc)
                if cc >= 2:
                    bg.append(fins[cc - 2])
                bg.extend(st_)
                fins.append(fn_)
            bg.append(fins[6])
            bg.append(fins[7])
            bgi = [0]

            def bg_step(n=1):
                for _ in range(n):
                    if bgi[0] < len(bg):
                        bg[bgi[0]]()
                        bgi[0] += 1

            blocks = [4, 5, 6, 7] + ([] if first_in_seq else [3, 2, 1, 0])
            seq = [(h, b) for h in range(8) for b in blocks]

            def kblock(h, b):
                if b >= 4:
                    return kTc[:, h, (b - 4) * 128:(b - 3) * 128], B_kc[h]
                return kTp1[:, h, b * 128:(b + 1) * 128], _bkp[h]

            def vblock(h, b):
                if b >= 4:
                    return Vc[:, b - 4, h * 128:(h + 1) * 128], B_vc[b - 4]
                return Vp1[:, b, h * 128:(h + 1) * 128], _bvp[b]

            def qrange(b):
                return max(0, 128 * b - 512), min(512, 128 * b + 128)

            def emit_S(k):
                h, b = seq[k]
                qs, qe = qrange(b)
                kap, kb = kblock(h, b)
                pi = k % 2
                S.add("pe", lambda e, kap=kap, h=h, qs=qs, qe=qe, pi=pi: e.matmul(psum[pi][:, 0:qe - qs], lhsT=kap, rhs=qT[:, h, qs:qe], start=True, stop=True),
                      reads=[kb, B_q[h]], writes=[B_ps[pi]])

            def load_M(h):
                mi = h % 2
                S.add("sp", lambda e, h=h, mi=mi: e.dma_start(out=Mh[mi][:], in_=MS[:, l * 8 + h, :]), writes=[B_M[mi]], dma_sem=sem_M[mi],
                      extra=[B_ms_all.w])

            def emit_PV(k):
                h, b = seq[k]
                qs, qe = qrange(b)
                n = qe - qs
                pi = k % 2
                us = qs - 128 * b + 512
                mi = h % 2
                et, etb = tmp_next()
                S.add("act", lambda e, et=et, pi=pi, n=n: e.activation(out=et[:, 0:n], in_=psum[pi][:, 0:n], func=AF.Exp, scale=SCALE),
                      reads=[B_ps[pi]], writes=[etb])
                S.add("dve", lambda e, et=et, pi=pi, n=n, us=us, mi=mi: e.tensor_tensor(out=ptb[pi][:, 0:n], in0=et[:, 0:n], in1=Mh[mi][:, us:us + n], op=ALU.mult),
                      reads=[etb, B_M[mi]], writes=[B_pt[pi]])
                vap, vb = vblock(h, b)
                po, pd = 2 + h % 2, 4 + h % 2
                first = (b == blocks[0])
                last = (b == blocks[-1])
                S.add("pe", lambda e, vap=vap, pi=pi, n=n, qs=qs, qe=qe, po=po, first=first, last=last: e.matmul(
                    psum[po][:, qs:qe], lhsT=vap, rhs=ptb[pi][:, 0:n], start=first, stop=last),
                    reads=[vb, B_pt[pi]], writes=[B_ps[po]], sig=False)
                S.add("pe", lambda e, pi=pi, n=n, qs=qs, qe=qe, pd=pd, first=first, last=last: e.matmul(
                    psum[pd][:, qs:qe], lhsT=ones_b[:], rhs=ptb[pi][:, 0:n], start=first, stop=last),
                    reads=[B_ones, B_pt[pi]], writes=[B_ps[pd]], sig=True)
                if last:
                    rd, rdb = tmp_next()
                    S.add("dve", lambda e, rd=rd, pd=pd: e.reciprocal(out=rd[:], in_=psum[pd][:]), reads=[B_ps[pd]], writes=[rdb])
                    S.add("dve", lambda e, rd=rd, po=po, h=h: e.tensor_tensor(out=xl[:, h, 0:T], in0=rd[:], in1=psum[po][:], op=ALU.mult),
                          reads=[rdb, B_ps[po]], writes=[B_xl[h]])
                    if h + 2 < 8:
                        load_M(h + 2)

            for jp in range(8):
                W_feat(jp)
            bg_step(2)
            for jp in range(8, 16):
                W_feat(jp)
                bg_step(1)
            if cut < 2:
                return
            for jq in range(2):
                for jpp in range(2):
                    V_unit(jq, jpp)
                    bg_step(2)
            if cut < 4:
                return
            load_M(0)
            load_M(1)
            emit_S(0)
            npairs = len(seq)
            nbg_left = len(bg) - bgi[0]
            every = max(1, npairs // max(1, nbg_left + 2))
            for k in range(npairs):
                if k + 1 < npairs:
                    emit_S(k + 1)
                emit_PV(k)
                if k % every == every - 1:
                    bg_step(1)
            while bgi[0] < len(bg):
                bg_step(1)
            if cut < 5:
                return
            S.add("sp", lambda e: e.dma_start(out=KD[l], in_=kTc[:]), reads=B_kc, writes=[B_kd[l]], dma_sem=sem_kst)
            S.add("sp", lambda e: e.dma_start(out=VD[l], in_=Vc[:]), reads=B_vc, writes=[B_vd[l]], dma_sem=sem_vst)
            if cut < 6:
                return
            rA, rAb = rms_stats([(gg[:, c, :], B_gg[c]) for c in range(8)], float(DLRU), 6)
            rB, rBb = rms_stats([(xl[:, c, 0:T], B_xl[c]) for c in range(8)], float(DLRU), 7)
            for c in range(8):
                S.add("dve", lambda e, c=c: e.scalar_tensor_tensor(out=hT[:, c, :], in0=gg[:, c, :], scalar=col(l, "ga", c), in1=rA[:], op0=ALU.mult, op1=ALU.mult),
                      reads=[B_gg[c], rAb, B_prm], writes=[B_h[c]])
            for c in range(8):
                S.add("dve", lambda e, c=c: e.scalar_tensor_tensor(out=hT[:, 8 + c, :], in0=xl[:, c, 0:T], scalar=col(l, "gb", c), in1=rB[:], op0=ALU.mult, op1=ALU.mult),
                      reads=[B_xl[c], rBb, B_prm], writes=[B_h[8 + c]])
            for jp in range(8):
                slot = useA(("out", l, jp))
                for which in range(2):
                    dc = jp * 2 + which
                    pi = dc % 2
                    for c in range(NC_):
                        S.add("pe", lambda e, slot=slot, which=which, c=c, pi=pi: e.matmul(
                            psum[pi][:], lhsT=ringA[slot][:, which, c, :], rhs=hT[:, c, :], start=(c == 0), stop=(c == NC_ - 1)),
                            reads=[B_rA[slot], B_h[c]], writes=[B_ps[pi]], sig=(c == NC_ - 1))
                    S.add("dve", lambda e, dc=dc, pi=pi: e.tensor_tensor(out=xT[:, dc, :], in0=xT[:, dc, :], in1=psum[pi][:], op=ALU.add),
                          reads=[B_ps[pi], B_x[dc]], writes=[B_x[dc]])
                loadA()

        for ti in range(n_tiles):
            sq, pos = tile_info(ti)
            tok0 = ti * T
            src = xT_d.rearrange("(c p) t -> p c t", p=128)
            for c4 in range(4):
                S.add("sp", lambda e, c4=c4, tok0=tok0: e.dma_start(out=xT[:, c4 * 4:(c4 + 1) * 4, :], in_=src[:, c4 * 4:(c4 + 1) * 4, tok0:tok0 + T]),
                      writes=B_x[c4 * 4:(c4 + 1) * 4], dma_sem=sem_x[c4])
            if pos == 0 and "mix" in phases and ti > 0:
                S.add("dve", lambda e: e.memset(hst[:], 0.0), writes=[b for l in range(L) for b in B_hst[l]])
                S.add("dve", lambda e: e.memset(ctail[:], 0.0), writes=[b for l in range(L) for b in B_ct[l]])
            for l in range(L):
                for ph in phases:
                    if ph == "ffn1":
                        ffn(l, 0)
                    elif ph == "ffn2":
                        ffn(l, 1)
                    else:
                        mixer(l, pos == 0)
            if final_norm:
                r, rb = rms_stats([(xT[:, c, :], B_x[c]) for c in range(NC_)], float(D), 6)
            alias_sync(B_h[0:4], B_stg)
            dst = out_d.rearrange("(c p) t -> p c t", p=128)
            for c in range(NC_):
                k = c % 2
                if final_norm:
                    g = prm[:, 2 * PC_PER_LAYER + c: 2 * PC_PER_LAYER + c + 1]
                    S.add("dve", lambda e, c=c, g=g, k=k, r=r: e.scalar_tensor_tensor(out=stg[k], in0=xT[:, c, :], scalar=g, in1=r[:], op0=ALU.mult, op1=ALU.mult),
                          reads=[B_x[c], rb, B_prm], writes=[B_stg[k]])
                else:
                    S.add("dve", lambda e, c=c, k=k: e.tensor_copy(out=stg[k], in_=xT[:, c, :]), reads=[B_x[c]], writes=[B_stg[k]])
                S.add("sp", lambda e, c=c, k=k, tok0=tok0: e.dma_start(out=dst[:, c, tok0:tok0 + T], in_=stg[k]),
                      reads=[B_stg[k]], writes=[B_out], dma_sem=sem_out[k])
            alias_sync(B_stg, B_h[0:4])
        final_waits = [(s_, s_.count) for s_ in sem_out]
        for e_ in ("pe", "act", "dve", "pool"):
            assert not S.pending_nosig[e_], e_

        with nc.Block() as block:
            @block.sync
            def _(e):
                S.emit("sp", e)
                for s_, v_ in final_waits:
                    e.wait_ge(s_.h, v_)

            @block.tensor
            def _(e):
                S.emit("pe", e)

            @block.scalar
            def _(e):
                S.emit("act", e)

            @block.vector
            def _(e):
                S.emit("dve", e)

            @block.gpsimd
            def _(e):
                S.emit("pool", e)
    return nc


def make_in_maps(inp, n_cores=NCORES, depth=2):
    x = np.asarray(inp["x"], np.float32)
    prm = pack_params(inp, depth)
    shared = {
        "prm": prm,
        "w_gate1": np.asarray(inp["ffn1_w_gate"], np.float32), "w_gate2": np.asarray(inp["ffn2_w_gate"], np.float32),
        "w_up1": np.asarray(inp["ffn1_w_up"], np.float32), "w_up2": np.asarray(inp["ffn2_w_up"], np.float32),
        "w_down1": np.asarray(inp["ffn1_w_down"], np.float32), "w_down2": np.asarray(inp["ffn2_w_down"], np.float32),
        "w_in": np.asarray(inp["w_in"], np.float32), "w_out": np.asarray(inp["w_out"], np.float32),
        "gate_a_w": np.asarray(inp["lru_gate_a_w"], np.float32), "gate_x_w": np.asarray(inp["lru_gate_x_w"], np.float32),
        "rel_bias": np.ascontiguousarray(np.asarray(inp["rel_bias"], np.float32).reshape(16, 257)),
    }
    maps = []
    for i in range(n_cores):
        xi = x[2 * i:2 * i + 2].reshape(TOK_PER_CORE, D)
        m = dict(shared)
        m["xT"] = np.ascontiguousarray(xi.T)
        maps.append(m)
    return maps


def kernel(**inputs):
    nc = build()
    maps = make_in_maps(inputs)
    res = run_bass_kernel_spmd(nc, maps, core_ids=list(range(NCORES)))
    outs = []
    for i in range(NCORES):
        oT = np.asarray(res.results[i]["outT"])
        outs.append(np.ascontiguousarray(oT.T).reshape(2, SEQ, D))
    return np.concatenate(outs, axis=0).astype(np.float32)
```
